# Optimizing a Trainium2 kernel written in Bass

```python
import jax
import jax.numpy as jnp
from jax import lax
import numpy as np

D_MODEL = 1024
BATCH = 1
SEQ = 16384
DEPTH = 4

HEAD_DIM = 64
D_FF = 2816
RMS_EPS = 1e-6
Q_BLOCK = 128
NEG_INF = -1e30

LRU_WIDTH = 768
LRU_BLOCKS = 6
LRU_BLOCK_W = LRU_WIDTH // LRU_BLOCKS
CONV_WIDTH = 4
LRU_C = 8.0

NSA_HEADS = 12
NSA_KV_HEADS = 3
NSA_GROUP = NSA_HEADS // NSA_KV_HEADS
NSA_Q_W = NSA_HEADS * HEAD_DIM
NSA_KV_W = NSA_KV_HEADS * HEAD_DIM
CMP_BLOCK = 32
CMP_STRIDE = 16
CMP_HIDDEN = 256
SEL_BLOCK = 64
SEL_TOPN = 16
WIN = 512
FORCE_SCORE = 1e9

DIL_GROUPS = ((128, 1), (512, 4), (2048, 16))
DIL_HEADS_PER_GROUP = 4
DIL_HEADS = DIL_HEADS_PER_GROUP * len(DIL_GROUPS)
DIL_W = DIL_HEADS * HEAD_DIM
DIL_OUT_W = DIL_HEADS_PER_GROUP * HEAD_DIM
DIL_PAD = max(w for w, _ in DIL_GROUPS)

IN_SPLITS = (LRU_WIDTH, LRU_WIDTH,
             NSA_Q_W,
             NSA_KV_W, NSA_KV_W,
             NSA_KV_W, NSA_KV_W,
             NSA_KV_W, NSA_KV_W,
             NSA_HEADS * 3,
             DIL_W, DIL_W, DIL_W,
             D_MODEL, D_MODEL, D_MODEL)
IN_WIDTH = sum(IN_SPLITS)
IN_OFFSETS = tuple(int(o) for o in np.cumsum(IN_SPLITS)[:-1])

kernel_name = 'hybrid_rglru_nsa_dilated_macaron'


def rms_norm(x, g):
    xf = x.astype(jnp.float32)
    y = xf * lax.rsqrt(jnp.mean(xf * xf, axis=-1, keepdims=True) + RMS_EPS)
    return (y * g.astype(jnp.float32)).astype(x.dtype)


def swiglu(x, w1, w3, w2):
    return (jax.nn.silu(x @ w1) * (x @ w3)) @ w2


def masked_softmax(s, mask):
    s = jnp.where(mask, s, NEG_INF)
    m = jnp.max(s, axis=-1, keepdims=True)
    e = jnp.where(mask, jnp.exp(s - m), 0.0)
    den = jnp.maximum(jnp.sum(e, axis=-1, keepdims=True), 1e-30)
    return e / den, m + jnp.log(den)


def causal_conv(x, w, b):
    S_ = x.shape[1]
    xp = jnp.pad(x, ((0, 0), (CONV_WIDTH - 1, 0), (0, 0)))
    out = b
    for j in range(CONV_WIDTH):
        out = out + w[j] * xp[:, j:j + S_]
    return out


def rg_lru(x, w_a, b_a, w_i, b_i, lam):
    B_, S_, _ = x.shape
    xf = x.astype(jnp.float32)
    xb = xf.reshape(B_, S_, LRU_BLOCKS, LRU_BLOCK_W)
    r = jax.nn.sigmoid(jnp.einsum('bshi,hij->bshj', xb, w_a.astype(jnp.float32)).reshape(B_, S_, LRU_WIDTH) + b_a)
    i = jax.nn.sigmoid(jnp.einsum('bshi,hij->bshj', xb, w_i.astype(jnp.float32)).reshape(B_, S_, LRU_WIDTH) + b_i)
    log_a = -LRU_C * r * jax.nn.softplus(-lam.astype(jnp.float32))
    a = jnp.exp(log_a)
    u = jnp.sqrt(-jnp.expm1(2.0 * log_a)) * (i * xf)

    def combine(c1, c2):
        a1, b1 = c1
        a2, b2 = c2
        return a1 * a2, a2 * b1 + b2

    _, h = lax.associative_scan(combine, (a, u), axis=1)
    return h.astype(x.dtype)


def nsa_attention(q, k_cmp, v_cmp, k_slc, v_slc, k_win, v_win, gate_logits,
                  cmp_pos_k, cmp_pos_v, cmp_k_w1, cmp_k_w2, cmp_v_w1, cmp_v_w2):
    B_, S_ = q.shape[0], q.shape[1]
    scale = HEAD_DIM ** -0.5
    q = q.reshape(B_, S_, NSA_KV_HEADS, NSA_GROUP, HEAD_DIM)
    gates = jax.nn.sigmoid(gate_logits.astype(jnp.float32)).reshape(B_, S_, NSA_KV_HEADS, NSA_GROUP, 3)

    def to_heads(t):
        return t.reshape(B_, S_, NSA_KV_HEADS, HEAD_DIM)

    n_cmp = (S_ - CMP_BLOCK) // CMP_STRIDE + 1
    cmp_idx = np.arange(n_cmp)[:, None] * CMP_STRIDE + np.arange(CMP_BLOCK)[None, :]

    def compress(kv, pos, w1, w2):
        blocks = to_heads(kv)[:, cmp_idx] + pos[None, None, :, None, :]
        blocks = jnp.transpose(blocks, (0, 1, 3, 2, 4)).reshape(B_, n_cmp, NSA_KV_HEADS, CMP_BLOCK * HEAD_DIM)
        return jax.nn.gelu(blocks @ w1) @ w2

    kc = compress(k_cmp, cmp_pos_k, cmp_k_w1, cmp_k_w2)
    vc = compress(v_cmp, cmp_pos_v, cmp_v_w1, cmp_v_w2)
    cmp_end = jnp.asarray(np.arange(n_cmp) * CMP_STRIDE + CMP_BLOCK - 1, jnp.int32)

    n_sel = S_ // SEL_BLOCK
    n_top = min(SEL_TOPN, n_sel)
    sel_of_cmp = jax.nn.one_hot(jnp.asarray(np.arange(n_cmp) * CMP_STRIDE // SEL_BLOCK, jnp.int32),
                                n_sel, dtype=jnp.float32)
    sel_blk = jnp.arange(n_sel, dtype=jnp.int32)
    ks_t = jnp.transpose(to_heads(k_slc), (0, 2, 1, 3))
    vs_t = jnp.transpose(to_heads(v_slc), (0, 2, 1, 3))
    gather_rows = jax.vmap(jax.vmap(lambda src, idx: src[idx]))

    kw_p = jnp.pad(to_heads(k_win), ((0, 0), (WIN, 0), (0, 0), (0, 0)))
    vw_p = jnp.pad(to_heads(v_win), ((0, 0), (WIN, 0), (0, 0), (0, 0)))

    def block(qb):
        start = qb * Q_BLOCK
        t = start + jnp.arange(Q_BLOCK, dtype=jnp.int32)
        qq = lax.dynamic_slice_in_dim(q, start, Q_BLOCK, axis=1)
        gg = lax.dynamic_slice_in_dim(gates, start, Q_BLOCK, axis=1)

        s_c = jnp.einsum('bqkgd,bckd->bkgqc', qq, kc).astype(jnp.float32) * scale
        p_c, _ = masked_softmax(s_c, cmp_end[None, :] <= t[:, None])
        o_c = jnp.einsum('bkgqc,bckd->bqkgd', p_c.astype(vc.dtype), vc)

        imp = jnp.einsum('bkgqc,cn->bkqn', p_c, sel_of_cmp)
        cur = (t // SEL_BLOCK)[:, None]
        imp = jnp.where((sel_blk[None, :] == cur) | (sel_blk[None, :] == 0), FORCE_SCORE,
                        jnp.where(sel_blk[None, :] > cur, -1.0, imp))
        _, top = lax.top_k(imp, n_top)
        n_keys = n_top * SEL_BLOCK
        kpos = (top[..., None] * SEL_BLOCK + jnp.arange(SEL_BLOCK, dtype=jnp.int32)).reshape(
            B_, NSA_KV_HEADS, Q_BLOCK * n_keys)
        k_sel = gather_rows(ks_t, kpos).reshape(B_, NSA_KV_HEADS, Q_BLOCK, n_keys, HEAD_DIM)
        v_sel = gather_rows(vs_t, kpos).reshape(B_, NSA_KV_HEADS, Q_BLOCK, n_keys, HEAD_DIM)
        kpos = kpos.reshape(B_, NSA_KV_HEADS, 1, Q_BLOCK, n_keys)
        s_s = jnp.einsum('bqkgd,bkqnd->bkgqn', qq, k_sel).astype(jnp.float32) * scale
        p_s, _ = masked_softmax(s_s, kpos <= t[:, None])
        o_s = jnp.einsum('bkgqn,bkqnd->bqkgd', p_s.astype(v_sel.dtype), v_sel)

        k_w = lax.dynamic_slice_in_dim(kw_p, start, WIN + Q_BLOCK, axis=1)
        v_w = lax.dynamic_slice_in_dim(vw_p, start, WIN + Q_BLOCK, axis=1)
        wpos = start - WIN + jnp.arange(WIN + Q_BLOCK, dtype=jnp.int32)
        dist = t[:, None] - wpos[None, :]
        s_w = jnp.einsum('bqkgd,bskd->bkgqs', qq, k_w).astype(jnp.float32) * scale
        p_w, _ = masked_softmax(s_w, (dist >= 0) & (dist < WIN) & (wpos[None, :] >= 0))
        o_w = jnp.einsum('bkgqs,bskd->bqkgd', p_w.astype(v_w.dtype), v_w)

        o = gg[..., 0:1] * o_c + gg[..., 1:2] * o_s + gg[..., 2:3] * o_w
        return o.reshape(B_, Q_BLOCK, NSA_Q_W).astype(q.dtype)

    out = lax.map(block, jnp.arange(S_ // Q_BLOCK, dtype=jnp.int32))
    return jnp.transpose(out, (1, 0, 2, 3)).reshape(B_, S_, NSA_Q_W)


def dilated_attention(q, k, v):
    B_, S_ = q.shape[0], q.shape[1]
    scale = HEAD_DIM ** -0.5
    n_grp = len(DIL_GROUPS)
    shp = (B_, S_, n_grp, DIL_HEADS_PER_GROUP, HEAD_DIM)
    q = q.reshape(shp)
    pad = ((0, 0), (DIL_PAD, 0), (0, 0), (0, 0), (0, 0))
    k_p = jnp.pad(k.reshape(shp), pad)
    v_p = jnp.pad(v.reshape(shp), pad)

    def block(qb):
        start = qb * Q_BLOCK
        t = start + jnp.arange(Q_BLOCK, dtype=jnp.int32)
        qq = lax.dynamic_slice_in_dim(q, start, Q_BLOCK, axis=1)
        outs, lses = [], []
        for gi, (window, dil) in enumerate(DIL_GROUPS):
            n_keys = window // dil + 1
            kpos = t[:, None] - dil * jnp.arange(n_keys, dtype=jnp.int32)[None, :]
            idx = (kpos + DIL_PAD).reshape(-1)
            kg = jnp.take(k_p[:, :, gi], idx, axis=1).reshape(B_, Q_BLOCK, n_keys, DIL_HEADS_PER_GROUP, HEAD_DIM)
            vg = jnp.take(v_p[:, :, gi], idx, axis=1).reshape(B_, Q_BLOCK, n_keys, DIL_HEADS_PER_GROUP, HEAD_DIM)
            s = jnp.einsum('bqhd,bqnhd->bhqn', qq[:, :, gi], kg).astype(jnp.float32) * scale
            p, lse = masked_softmax(s, kpos >= 0)
            outs.append(jnp.einsum('bhqn,bqnhd->bqhd', p.astype(vg.dtype), vg))
            lses.append(jnp.transpose(lse[..., 0], (0, 2, 1)))
        wts = jax.nn.softmax(jnp.stack(lses, axis=-1), axis=-1)
        o = jnp.sum(jnp.stack(outs, axis=-1) * wts[:, :, :, None, :], axis=-1)
        return o.reshape(B_, Q_BLOCK, DIL_OUT_W).astype(q.dtype)

    out = lax.map(block, jnp.arange(S_ // Q_BLOCK, dtype=jnp.int32))
    return jnp.transpose(out, (1, 0, 2, 3)).reshape(B_, S_, DIL_OUT_W)


def hybrid_mixer(h, w_in, conv_w, conv_b, lru_wa, lru_ba, lru_wi, lru_bi, lru_lambda,
                 cmp_pos_k, cmp_pos_v, cmp_k_w1, cmp_k_w2, cmp_v_w1, cmp_v_w2,
                 w_up_a, w_up_b, w_up_c, w_out):
    (a_x, a_gate, b_q, b_kc, b_vc, b_ks, b_vs, b_kw, b_vw, b_gate,
     c_q, c_k, c_v, m_a, m_b, m_c) = jnp.split(h @ w_in, list(IN_OFFSETS), axis=-1)
    y_a = rg_lru(causal_conv(a_x, conv_w, conv_b), lru_wa, lru_ba, lru_wi, lru_bi, lru_lambda)
    y_a = (y_a * jax.nn.gelu(a_gate)) @ w_up_a
    y_b = nsa_attention(b_q, b_kc, b_vc, b_ks, b_vs, b_kw, b_vw, b_gate,
                        cmp_pos_k, cmp_pos_v, cmp_k_w1, cmp_k_w2, cmp_v_w1, cmp_v_w2) @ w_up_b
    y_c = dilated_attention(c_q, c_k, c_v) @ w_up_c
    merged = jax.nn.sigmoid(m_a) * y_a + jax.nn.sigmoid(m_b) * y_b + jax.nn.sigmoid(m_c) * y_c
    return merged @ w_out


def _normal(key, shape, scale):
    return jax.random.normal(key, shape, jnp.float32) * scale


def setup_inputs(seed: int = 0) -> dict:
    key = jax.random.key(seed)
    k = jax.random.split(key, 32)
    L = DEPTH
    a_lo, a_hi = 0.9 ** (1.0 / LRU_C), 0.999 ** (1.0 / LRU_C)
    a0 = jax.random.uniform(k[13], (L, LRU_WIDTH), jnp.float32, a_lo, a_hi)
    flat = CMP_BLOCK * HEAD_DIM
    return {
        'x': _normal(k[0], (BATCH, SEQ, D_MODEL), 1.0),
        'ffn1_norm': 1.0 + _normal(k[1], (L, D_MODEL), 0.01),
        'ffn1_w1': _normal(k[2], (L, D_MODEL, D_FF), D_MODEL ** -0.5),
        'ffn1_w3': _normal(k[3], (L, D_MODEL, D_FF), D_MODEL ** -0.5),
        'ffn1_w2': _normal(k[4], (L, D_FF, D_MODEL), D_FF ** -0.5),
        'mix_norm': 1.0 + _normal(k[5], (L, D_MODEL), 0.01),
        'w_in': _normal(k[6], (L, D_MODEL, IN_WIDTH), D_MODEL ** -0.5),
        'conv_w': _normal(k[7], (L, CONV_WIDTH, LRU_WIDTH), CONV_WIDTH ** -0.5),
        'conv_b': _normal(k[8], (L, LRU_WIDTH), 0.01),
        'lru_wa': _normal(k[9], (L, LRU_BLOCKS, LRU_BLOCK_W, LRU_BLOCK_W), LRU_BLOCK_W ** -0.5),
        'lru_ba': _normal(k[10], (L, LRU_WIDTH), 0.01),
        'lru_wi': _normal(k[11], (L, LRU_BLOCKS, LRU_BLOCK_W, LRU_BLOCK_W), LRU_BLOCK_W ** -0.5),
        'lru_bi': _normal(k[12], (L, LRU_WIDTH), 0.01),
        'lru_lambda': jnp.log(a0) - jnp.log1p(-a0),
        'cmp_pos_k': _normal(k[14], (L, CMP_BLOCK, HEAD_DIM), 0.1),
        'cmp_pos_v': _normal(k[15], (L, CMP_BLOCK, HEAD_DIM), 0.1),
        'cmp_k_w1': _normal(k[16], (L, flat, CMP_HIDDEN), flat ** -0.5),
        'cmp_k_w2': _normal(k[17], (L, CMP_HIDDEN, HEAD_DIM), CMP_HIDDEN ** -0.5),
        'cmp_v_w1': _normal(k[18], (L, flat, CMP_HIDDEN), flat ** -0.5),
        'cmp_v_w2': _normal(k[19], (L, CMP_HIDDEN, HEAD_DIM), CMP_HIDDEN ** -0.5),
        'w_up_a': _normal(k[20], (L, LRU_WIDTH, D_MODEL), LRU_WIDTH ** -0.5),
        'w_up_b': _normal(k[21], (L, NSA_Q_W, D_MODEL), NSA_Q_W ** -0.5),
        'w_up_c': _normal(k[22], (L, DIL_OUT_W, D_MODEL), DIL_OUT_W ** -0.5),
        'w_out': _normal(k[23], (L, D_MODEL, D_MODEL), D_MODEL ** -0.5),
        'ffn2_norm': 1.0 + _normal(k[24], (L, D_MODEL), 0.01),
        'ffn2_w1': _normal(k[25], (L, D_MODEL, D_FF), D_MODEL ** -0.5),
        'ffn2_w3': _normal(k[26], (L, D_MODEL, D_FF), D_MODEL ** -0.5),
        'ffn2_w2': _normal(k[27], (L, D_FF, D_MODEL), D_FF ** -0.5),
        'final_norm': 1.0 + _normal(k[28], (D_MODEL,), 0.01),
    }


def reference(x, ffn1_norm, ffn1_w1, ffn1_w3, ffn1_w2, mix_norm, w_in, conv_w, conv_b,
              lru_wa, lru_ba, lru_wi, lru_bi, lru_lambda, cmp_pos_k, cmp_pos_v,
              cmp_k_w1, cmp_k_w2, cmp_v_w1, cmp_v_w2, w_up_a, w_up_b, w_up_c, w_out,
              ffn2_norm, ffn2_w1, ffn2_w3, ffn2_w2, final_norm):
    for l in range(DEPTH):
        x = x + 0.5 * swiglu(rms_norm(x, ffn1_norm[l]), ffn1_w1[l], ffn1_w3[l], ffn1_w2[l])
        x = x + hybrid_mixer(rms_norm(x, mix_norm[l]), w_in[l], conv_w[l], conv_b[l],
                             lru_wa[l], lru_ba[l], lru_wi[l], lru_bi[l], lru_lambda[l],
                             cmp_pos_k[l], cmp_pos_v[l], cmp_k_w1[l], cmp_k_w2[l],
                             cmp_v_w1[l], cmp_v_w2[l], w_up_a[l], w_up_b[l], w_up_c[l], w_out[l])
        x = x + 0.5 * swiglu(rms_norm(x, ffn2_norm[l]), ffn2_w1[l], ffn2_w3[l], ffn2_w2[l])
    return rms_norm(x, final_norm)
```

```python
import numpy as np
import ml_dtypes
from contextlib import ExitStack
import concourse.bass as bass
import concourse.mybir as mybir
from concourse.bass_utils import run_bass_kernel_spmd

F32 = mybir.dt.float32
BF16 = mybir.dt.bfloat16
AF = mybir.ActivationFunctionType
ALU = mybir.AluOpType
AX = mybir.AxisListType
NPBF = ml_dtypes.bfloat16


class Buf:
    __slots__ = ("name", "lw", "rd", "sem", "dcnt", "plw", "prd")

    def __init__(self, name):
        self.name = name
        self.lw = None
        self.rd = {}
        self.sem = None
        self.dcnt = 0
        self.plw = None
        self.prd = {}


class Prog:
    ENG = ("pe", "dve", "act", "pool", "sp")

    def __init__(self, nc, es, tag=""):
        self.nc = nc
        self.es = es
        self.tag = tag
        self.ops = {e: [] for e in self.ENG}
        self.cnt = {e: 0 for e in self.ENG}
        self.sem = {e: es.enter_context(nc.semaphore(f"s{tag}_{e}"))
                    for e in ("pe", "dve", "act", "pool")}
        self.waited = {e: {} for e in self.ENG}
        self.nsem = 0
        self.outbufs = []
        self.seq = 0

    def _need(self, eng, tok, waits):
        if tok is None:
            return
        sem, val, src = tok
        if src == "pe" and eng == "pe":
            return
        k = id(sem)
        if self.waited[eng].get(k, (None, 0))[1] >= val:
            return
        if k not in waits or waits[k][1] < val:
            waits[k] = (sem, val)

    def _commit_waits(self, eng, waits):
        for k, sv in waits.items():
            self.waited[eng][k] = sv
        return list(waits.values())

    def op(self, eng, fn, reads=(), writes=()):
        waits = {}
        for b in reads:
            self._need(eng, b.lw, waits)
        for b in writes:
            self._need(eng, b.lw, waits)
            for t in b.rd.values():
                self._need(eng, t, waits)
        self.cnt[eng] += 1
        tok = (self.sem[eng], self.cnt[eng], eng)
        wl = self._commit_waits(eng, waits)
        ws = set(id(b) for b in writes)
        for b in reads:
            if id(b) not in ws:
                b.rd[id(tok[0])] = tok
        for b in writes:
            b.lw = tok
            b.rd = {}
            b.plw = None
            b.prd = {}
        self.seq += 1
        self.ops[eng].append((wl, fn, (self.sem[eng], 1), self.seq))

    def dma(self, q, out, in_, dst, src=None, **kw):
        waits = {}
        if src is not None:
            self._need(q, src.lw, waits)
        cont = dst.lw is not None and dst.sem is not None and dst.lw[0] is dst.sem and not dst.rd
        if cont:
            self._need(q, dst.plw, waits)
            for t in dst.prd.values():
                self._need(q, t, waits)
        else:
            self._need(q, dst.lw, waits)
            for t in dst.rd.values():
                self._need(q, t, waits)
            dst.plw = dst.lw
            dst.prd = dict(dst.rd)
        if dst.sem is None:
            self.nsem += 1
            dst.sem = self.es.enter_context(self.nc.semaphore(f"d{self.tag}_{self.nsem}"))
        dst.dcnt += 16
        tok = (dst.sem, dst.dcnt, "dma")
        wl = self._commit_waits(q, waits)
        dst.lw = tok
        dst.rd = {}
        if src is not None:
            src.rd[id(dst.sem)] = tok

        def fn(e, out=out, in_=in_, kw=kw):
            return e.dma_start(out=out, in_=in_, **kw)
        self.seq += 1
        self.ops[q].append((wl, fn, (dst.sem, 16), self.seq))

    def finish(self, outbufs):
        waits = {}
        for b in outbufs:
            self._need("sp", b.lw, waits)
        for e in ("pe", "dve", "act", "pool"):
            if self.cnt[e]:
                self._need("sp", (self.sem[e], self.cnt[e], e), waits)
        wl = self._commit_waits("sp", waits)
        self.seq += 1
        self.ops["sp"].append((wl, None, None, self.seq))

    def emit(self, seg_limit=6000):
        nc = self.nc
        ops = self.ops
        allops = []
        for e in self.ENG:
            for (wl, fn, inc, seq) in ops[e]:
                allops.append((seq, e, len(wl) + (1 if fn is not None else 0)))
        allops.sort()
        cuts = []
        cnt = {e: 0 for e in self.ENG}
        for seq, e, n in allops:
            if cnt[e] + n > seg_limit:
                cuts.append(seq)
                cnt = {k: 0 for k in self.ENG}
            cnt[e] += n
        bounds = [0] + cuts + [self.seq + 1]
        pos = {e: 0 for e in self.ENG}
        for si in range(len(bounds) - 1):
            hi = bounds[si + 1]
            seg = {}
            for e in self.ENG:
                j = pos[e]
                lst = ops[e]
                k = j
                while k < len(lst) and lst[k][3] < hi:
                    k += 1
                seg[e] = lst[j:k]
                pos[e] = k

            def replay(name, e, seg=seg):
                for wl, fn, inc, seq in seg[name]:
                    for s_, v in wl:
                        e.wait_ge(s_, v)
                    if fn is not None:
                        ins = fn(e)
                        ins.then_inc(inc[0], inc[1])

            with nc.Block() as block:
                @block.sync
                def _(e):
                    replay("sp", e)

                @block.tensor
                def _(e):
                    replay("pe", e)

                @block.vector
                def _(e):
                    replay("dve", e)

                @block.scalar
                def _(e):
                    replay("act", e)

                @block.gpsimd
                def _(e):
                    replay("pool", e)


D = 1024
DFF = 2816
NF = DFF // 128
TOK = 2048
PASS = 1024
NTT = PASS // 512
EPS = 1e-6

FM32_CH = 36
FM16_CH = 24
TM_W = 1188


def chunked(W):
    K, M = W.shape
    return np.ascontiguousarray(
        W.reshape(K // 128, 128, M // 128, 128).transpose(2, 1, 0, 3).reshape(M // 128, 128, K))


def vec_pk(g):
    return np.ascontiguousarray(g.reshape(-1, 128).T)


class Ctx:
    pass


def alloc_common(nc, es, P):
    c = Ctx()
    c.nc, c.es, c.P = nc, es, P
    sb = lambda name, shape, dt: es.enter_context(nc.sbuf_tensor(name, shape, dt))
    c.sb = sb
    c.x = sb("x_sb", [128, 8, PASS], F32)
    c.bx = [[Buf(f"x{k}_{t}") for t in range(NTT)] for k in range(8)]
    c.xn = sb("xn_sb", [128, 8, PASS], BF16)
    c.bxn = [[Buf(f"xn{k}_{t}") for t in range(NTT)] for k in range(8)]
    c.sq = [sb(f"sq{i}", [128, 512], BF16) for i in range(2)]
    c.bsq = [Buf(f"sq{i}") for i in range(2)]
    c.rstd = sb("rstd", [128, 512], F32)
    c.brstd = Buf("rstd")
    c.ones = sb("ones", [128, 128], BF16)
    c.bones = Buf("ones")
    P.op("pool", lambda e: e.memset(c.ones[:], 1.0), writes=[c.bones])
    c.ps = [es.enter_context(nc.psum_tensor(f"ps{i}", [128, 512], F32)) for i in range(7)]
    c.bps = [Buf(f"ps{i}") for i in range(7)]
    c.psi = 0
    c.sqi = 0
    return c


def next_ps(c):
    i = c.psi
    c.psi = (i + 1) % len(c.ps)
    return c.ps[i], c.bps[i]


def emit_rmsnorm(c, g_sb, bg):
    P = c.P
    for t in range(NTT):
        ts = slice(t * 512, (t + 1) * 512)
        pss, bpss = next_ps(c)
        for k in range(8):
            i = c.sqi
            c.sqi = (i + 1) % 2
            sq, bsq = c.sq[i], c.bsq[i]
            P.op("act", lambda e, sq=sq, k=k, ts=ts: e.activation(out=sq[:], in_=c.x[:, k, ts], func=AF.Square),
                 reads=[c.bx[k][t]], writes=[bsq])
            P.op("pe", lambda e, sq=sq, k=k, pss=pss: e.matmul(pss[:], c.ones[:], sq[:], start=(k == 0), stop=(k == 7)),
                 reads=[bsq, c.bones], writes=[bpss])
        P.op("dve", lambda e, pss=pss: e.tensor_scalar(out=c.rstd[:], in0=pss[:], scalar1=1.0 / D, scalar2=EPS,
                                                         op0=ALU.mult, op1=ALU.add),
             reads=[bpss], writes=[c.brstd])
        P.op("act", lambda e: e.activation(out=c.rstd[:], in_=c.rstd[:], func=AF.Sqrt),
             reads=[c.brstd], writes=[c.brstd])
        P.op("dve", lambda e: e.reciprocal(out=c.rstd[:], in_=c.rstd[:]),
             reads=[c.brstd], writes=[c.brstd])
        for k in range(8):
            P.op("dve", lambda e, k=k, ts=ts: e.scalar_tensor_tensor(
                out=c.xn[:, k, ts], in0=c.x[:, k, ts], scalar=g_sb[:, k:k + 1], in1=c.rstd[:],
                op0=ALU.mult, op1=ALU.mult),
                reads=[c.bx[k][t], c.brstd, bg], writes=[c.bxn[k][t]])


def alloc_ffn(c):
    sb = c.sb
    c.g = sb("g_sb", [128, NF, PASS], BF16)
    c.bg_ = [[Buf(f"g{f}_{t}") for t in range(NTT)] for f in range(NF)]
    c.w13 = [sb(f"w13_{i}", [128, 2048], BF16) for i in range(3)]
    c.bw13 = [Buf(f"w13_{i}") for i in range(3)]
    c.w2 = [sb(f"w2_{i}", [128, DFF], BF16) for i in range(2)]
    c.bw2 = [Buf(f"w2_{i}") for i in range(2)]
    c.s = [sb(f"s_{i}", [128, 512], F32) for i in range(2)]
    c.bs = [Buf(f"s_{i}") for i in range(2)]
    c.w13i = 0
    c.w2i = 0
    c.si = 0


def emit_ffn(c, w13_d, w2_d):
    P = c.P
    for f in range(NF):
        i = c.w13i
        c.w13i = (i + 1) % 3
        w, bw = c.w13[i], c.bw13[i]
        P.dma("pool", w[:], w13_d[f], bw)
        for t in range(NTT):
            ts = slice(t * 512, (t + 1) * 512)
            p1, bp1 = next_ps(c)
            p3, bp3 = next_ps(c)
            for k in range(8):
                P.op("pe", lambda e, w=w, k=k, ts=ts, p1=p1: e.matmul(
                    p1[:], w[:, k * 128:(k + 1) * 128], c.xn[:, k, ts], start=(k == 0), stop=(k == 7)),
                    reads=[bw, c.bxn[k][t]], writes=[bp1])
            for k in range(8):
                P.op("pe", lambda e, w=w, k=k, ts=ts, p3=p3: e.matmul(
                    p3[:], w[:, 1024 + k * 128:1024 + (k + 1) * 128], c.xn[:, k, ts], start=(k == 0), stop=(k == 7)),
                    reads=[bw, c.bxn[k][t]], writes=[bp3])
            si = c.si
            c.si = (si + 1) % 2
            s, bs = c.s[si], c.bs[si]
            P.op("act", lambda e, s=s, p1=p1: e.activation(out=s[:], in_=p1[:], func=AF.Silu),
                 reads=[bp1], writes=[bs])
            P.op("dve", lambda e, s=s, p3=p3, f=f, ts=ts: e.tensor_tensor(
                out=c.g[:, f, ts], in0=s[:], in1=p3[:], op=ALU.mult),
                reads=[bs, bp3], writes=[c.bg_[f][t]])
    for d in range(8):
        i = c.w2i
        c.w2i = (i + 1) % 2
        w, bw = c.w2[i], c.bw2[i]
        P.dma("pool", w[:], w2_d[d], bw)
        for t in range(NTT):
            ts = slice(t * 512, (t + 1) * 512)
            po, bpo = next_ps(c)
            for f in range(NF):
                P.op("pe", lambda e, w=w, f=f, ts=ts, po=po: e.matmul(
                    po[:], w[:, f * 128:(f + 1) * 128], c.g[:, f, ts], start=(f == 0), stop=(f == NF - 1)),
                    reads=[bw, c.bg_[f][t]], writes=[bpo])
            P.op("dve", lambda e, d=d, ts=ts, po=po: e.scalar_tensor_tensor(
                out=c.x[:, d, ts], in0=po[:], scalar=0.5, in1=c.x[:, d, ts], op0=ALU.mult, op1=ALU.add),
                reads=[bpo], writes=[c.bx[d][t]])


def load_x(c, xT_d, t0, bsrc=None):
    for k in range(8):
        for t in range(NTT):
            c.P.dma("sp", c.x[:, k, t * 512:(t + 1) * 512],
                    xT_d[k * 128:(k + 1) * 128, t0 + t * 512:t0 + (t + 1) * 512], c.bx[k][t], src=bsrc)


def store_x(c, xT_d, t0, bdst):
    for k in range(8):
        for t in range(NTT):
            c.P.dma("sp", xT_d[k * 128:(k + 1) * 128, t0 + t * 512:t0 + (t + 1) * 512],
                    c.x[:, k, t * 512:(t + 1) * 512], bdst, src=c.bx[k][t])


def build_A():
    nc = bass.Bass("TRN2", target_bir_lowering=False)
    dt = lambda name, shape, dtype, kind: nc.dram_tensor(name, shape, dtype, kind=kind).ap()
    xT = dt("xT", [D, TOK], F32, "ExternalInput")
    w13 = dt("w13", [NF, 128, 2048], F32, "ExternalInput")
    w2 = dt("w2", [8, 128, DFF], F32, "ExternalInput")
    gf = dt("gf", [128, 8], F32, "ExternalInput")
    gm = dt("gm", [128, 8], F32, "ExternalInput")
    wfm = dt("wfm", [FM32_CH + FM16_CH, 128, 1024], F32, "ExternalInput")
    wtm = dt("wtm", [128, 8 * TM_W], F32, "ExternalInput")
    x1T = dt("x1T", [D, TOK], F32, "ExternalOutput")
    pfm32 = dt("pfm32", [FM32_CH * 128, TOK], F32, "ExternalOutput")
    pfm16 = dt("pfm16", [FM16_CH * 128, TOK], BF16, "ExternalOutput")
    ptm16 = dt("ptm16", [TOK, 1152], BF16, "ExternalOutput")
    pgate = dt("pgate", [TOK, 36], F32, "ExternalOutput")
    with ExitStack() as es:
        P = Prog(nc, es)
        c = alloc_common(nc, es, P)
        alloc_ffn(c)
        sb = c.sb
        gf_sb = sb("gf_sb", [128, 8], F32)
        gm_sb = sb("gm_sb", [128, 8], F32)
        bgf, bgm = Buf("gf"), Buf("gm")
        P.dma("sp", gf_sb[:], gf[:, :], bgf)
        P.dma("sp", gm_sb[:], gm[:, :], bgm)
        wtm_sb = sb("wtm_sb", [128, 8 * TM_W], BF16)
        bwtm = Buf("wtm")
        for k in range(8):
            P.dma("pool", wtm_sb[:, k * TM_W:(k + 1) * TM_W], wtm[:, k * TM_W:(k + 1) * TM_W], bwtm)
        wf = [sb(f"wf_{i}", [128, 1024], BF16) for i in range(3)]
        bwf = [Buf(f"wf_{i}") for i in range(3)]
        o32 = [sb(f"o32_{i}", [128, PASS], F32) for i in range(2)]
        bo32 = [Buf(f"o32_{i}") for i in range(2)]
        o16 = [sb(f"o16_{i}", [128, PASS], BF16) for i in range(2)]
        bo16 = [Buf(f"o16_{i}") for i in range(2)]
        otm = [sb(f"otm_{i}", [128, 1152], BF16) for i in range(2)]
        botm = [Buf(f"otm_{i}") for i in range(2)]
        ogt = [sb(f"ogt_{i}", [128, 36], F32) for i in range(2)]
        bogt = [Buf(f"ogt_{i}") for i in range(2)]
        bout = [Buf("o_x1"), Buf("o_fm32"), Buf("o_fm16"), Buf("o_tm"), Buf("o_gate")]
        for ps_ in range(TOK // PASS):
            t0 = ps_ * PASS
            load_x(c, xT, t0)
            emit_rmsnorm(c, gf_sb, bgf)
            emit_ffn(c, w13, w2)
            store_x(c, x1T, t0, bout[0])
            emit_rmsnorm(c, gm_sb, bgm)
            for ch in range(FM32_CH + FM16_CH):
                i = ch % 3
                w, bw = wf[i], bwf[i]
                P.dma("pool", w[:], wfm[ch], bw)
                is32 = ch < FM32_CH
                j = ch % 2
                ot, bot = (o32[j], bo32[j]) if is32 else (o16[j], bo16[j])
                if ch < 6:
                    func = AF.Copy
                elif ch < 12:
                    func = AF.Gelu_apprx_tanh
                elif ch < 36:
                    func = AF.Sigmoid
                else:
                    func = AF.Copy
                for t in range(NTT):
                    ts = slice(t * 512, (t + 1) * 512)
                    pp, bpp = next_ps(c)
                    for k in range(8):
                        P.op("pe", lambda e, w=w, k=k, ts=ts, pp=pp: e.matmul(
                            pp[:], w[:, k * 128:(k + 1) * 128], c.xn[:, k, ts], start=(k == 0), stop=(k == 7)),
                            reads=[bw, c.bxn[k][t]], writes=[bpp])
                    if func == AF.Copy and (ch + t) % 2 == 0:
                        P.op("dve", lambda e, ot=ot, pp=pp, ts=ts: e.tensor_copy(out=ot[:, ts], in_=pp[:]),
                             reads=[bpp], writes=[bot])
                    else:
                        P.op("act", lambda e, ot=ot, pp=pp, ts=ts, func=func: e.activation(
                            out=ot[:, ts], in_=pp[:], func=func), reads=[bpp], writes=[bot])
                if is32:
                    P.dma("sp", pfm32[ch * 128:(ch + 1) * 128, t0:t0 + PASS], ot[:], bout[1], src=bot)
                else:
                    c2 = ch - FM32_CH
                    P.dma("sp", pfm16[c2 * 128:(c2 + 1) * 128, t0:t0 + PASS], ot[:], bout[2], src=bot)
            for tb in range(PASS // 128):
                j = tb % 2
                t, off = divmod(tb * 128, 512)
                for (c0, cw) in ((0, 512), (512, 512), (1024, 164)):
                    pp, bpp = next_ps(c)
                    for k in range(8):
                        P.op("pe", lambda e, k=k, tb=tb, c0=c0, cw=cw, pp=pp: e.matmul(
                            pp[:, 0:cw], c.xn[:, k, tb * 128:(tb + 1) * 128],
                            wtm_sb[:, k * TM_W + c0:k * TM_W + c0 + cw], start=(k == 0), stop=(k == 7)),
                            reads=[bwtm, c.bxn[k][t]], writes=[bpp])
                    if c0 < 1024:
                        P.op("dve", lambda e, j=j, c0=c0, cw=cw, pp=pp: e.tensor_copy(
                            out=otm[j][:, c0:c0 + cw], in_=pp[:, 0:cw]), reads=[bpp], writes=[botm[j]])
                    else:
                        P.op("dve", lambda e, j=j, c0=c0, pp=pp: e.tensor_copy(
                            out=otm[j][:, c0:c0 + 128], in_=pp[:, 0:128]), reads=[bpp], writes=[botm[j]])
                        P.op("act", lambda e, j=j, pp=pp: e.activation(
                            out=ogt[j][:], in_=pp[:, 128:164], func=AF.Sigmoid), reads=[bpp], writes=[bogt[j]])
                r0 = t0 + tb * 128
                P.dma("sp", ptm16[r0:r0 + 128, :], otm[j][:], bout[3], src=botm[j])
                P.dma("sp", pgate[r0:r0 + 128, :], ogt[j][:], bout[4], src=bogt[j])
        P.finish(bout)
        P.emit()
    return nc


def w_in_layout(w_in):
    o = {}
    names = ["a_x", "a_gate", "b_q", "b_kc", "b_vc", "b_ks", "b_vs", "b_kw", "b_vw", "b_gate",
             "c_q", "c_k", "c_v", "m_a", "m_b", "m_c"]
    sizes = [768, 768, 768, 192, 192, 192, 192, 192, 192, 36, 768, 768, 768, 1024, 1024, 1024]
    off = 0
    for n, s in zip(names, sizes):
        o[n] = w_in[:, off:off + s]
        off += s
    fm = np.concatenate([o["a_x"], o["a_gate"], o["m_a"], o["m_b"], o["m_c"],
                         o["b_q"], o["b_kc"], o["b_vc"], o["b_ks"], o["b_kw"], o["c_q"], o["c_k"]], axis=1)
    tm = np.concatenate([o["b_vs"], o["b_vw"], o["c_v"], o["b_gate"]], axis=1)
    assert fm.shape[1] == 60 * 128 and tm.shape[1] == TM_W
    wfm = chunked(fm)
    wtm = np.ascontiguousarray(tm.reshape(8, 128, TM_W).transpose(1, 0, 2).reshape(128, 8 * TM_W))
    return wfm, wtm


def ffn_layout(w1, w3, w2):
    w13 = np.concatenate([chunked(w1), chunked(w3)], axis=2)
    return np.ascontiguousarray(w13), chunked(w2)


def emit_final_norm(c, g_sb, bg):
    P = c.P
    for t in range(NTT):
        ts = slice(t * 512, (t + 1) * 512)
        pss, bpss = next_ps(c)
        for k in range(8):
            i = c.sqi
            c.sqi = (i + 1) % 2
            sq, bsq = c.sq[i], c.bsq[i]
            P.op("act", lambda e, sq=sq, k=k, ts=ts: e.activation(out=sq[:], in_=c.x[:, k, ts], func=AF.Square),
                 reads=[c.bx[k][t]], writes=[bsq])
            P.op("pe", lambda e, sq=sq, k=k, pss=pss: e.matmul(pss[:], c.ones[:], sq[:], start=(k == 0), stop=(k == 7)),
                 reads=[bsq, c.bones], writes=[bpss])
        P.op("dve", lambda e, pss=pss: e.tensor_scalar(out=c.rstd[:], in0=pss[:], scalar1=1.0 / D, scalar2=EPS,
                                                         op0=ALU.mult, op1=ALU.add),
             reads=[bpss], writes=[c.brstd])
        P.op("act", lambda e: e.activation(out=c.rstd[:], in_=c.rstd[:], func=AF.Sqrt),
             reads=[c.brstd], writes=[c.brstd])
        P.op("dve", lambda e: e.reciprocal(out=c.rstd[:], in_=c.rstd[:]),
             reads=[c.brstd], writes=[c.brstd])
        for k in range(8):
            P.op("dve", lambda e, k=k, ts=ts: e.scalar_tensor_tensor(
                out=c.x[:, k, ts], in0=c.x[:, k, ts], scalar=g_sb[:, k:k + 1], in1=c.rstd[:],
                op0=ALU.mult, op1=ALU.mult),
                reads=[c.brstd, bg], writes=[c.bx[k][t]])


def build_C(last):
    nc = bass.Bass("TRN2", target_bir_lowering=False)
    dt = lambda name, shape, dtype, kind: nc.dram_tensor(name, shape, dtype, kind=kind).ap()
    xT = dt("xT", [D, TOK], F32, "ExternalInput")
    pfm32 = dt("pfm32", [FM32_CH * 128, TOK], F32, "ExternalInput")
    hloc = dt("hloc", [768, TOK], F32, "ExternalInput")
    pcum = dt("pcum", [768, TOK], F32, "ExternalInput")
    ends = dt("ends", [128, 6 * 2 * 8], F32, "ExternalInput")
    csel = dt("csel", [128, 8], F32, "ExternalInput")
    obT = dt("obT", [768, TOK], BF16, "ExternalInput")
    odT = dt("odT", [256, TOK], BF16, "ExternalInput")
    wup = dt("wup", [8, 128, 14 * 128], F32, "ExternalInput")
    wout = dt("wout", [8, 128, 1024], F32, "ExternalInput")
    w13 = dt("w13", [NF, 128, 2048], F32, "ExternalInput")
    w2 = dt("w2", [8, 128, DFF], F32, "ExternalInput")
    gf = dt("gf", [128, 8], F32, "ExternalInput")
    gl = dt("gl", [128, 8], F32, "ExternalInput")
    x2T = dt("x2T", [D, TOK], F32, "ExternalOutput")
    with ExitStack() as es:
        P = Prog(nc, es)
        c = alloc_common(nc, es, P)
        alloc_ffn(c)
        sb = c.sb
        gf_sb = sb("gf_sb", [128, 8], F32)
        gl_sb = sb("gl_sb", [128, 8], F32)
        bgf, bgl = Buf("gf"), Buf("gl")
        P.dma("sp", gf_sb[:], gf[:, :], bgf)
        P.dma("sp", gl_sb[:], gl[:, :], bgl)
        ends_sb = sb("ends_sb", [128, 96], F32)
        csel_sb = sb("csel_sb", [128, 8], F32)
        bends, bcsel = Buf("ends"), Buf("csel")
        P.dma("sp", ends_sb[:], ends[:, :], bends)
        P.dma("sp", csel_sb[:], csel[:, :], bcsel)
        car = sb("car", [128, 8], F32)
        bcar = Buf("car")
        cin = sb("cin", [128, 6], F32)
        bcin = Buf("cin")
        for ch in range(6):
            o = ch * 16
            P.op("dve", lambda e, o=o: e.tensor_tensor_scan(
                out=car[:], data0=ends_sb[:, o:o + 8], data1=ends_sb[:, o + 8:o + 16], initial=0.0,
                op0=ALU.mult, op1=ALU.add), reads=[bends], writes=[bcar])
            P.op("dve", lambda e: e.tensor_tensor(out=car[:], in0=car[:], in1=csel_sb[:], op=ALU.mult),
                 reads=[bcar, bcsel], writes=[bcar])
            P.op("dve", lambda e, ch=ch: e.reduce_sum(out=cin[:, ch:ch + 1], in_=car[:], axis=AX.X),
                 reads=[bcar], writes=[bcin])
        lt = [[sb(f"lt{j}_{i}", [128, 512], F32) for i in range(2)] for j in range(3)]
        blt = [[Buf(f"lt{j}_{i}") for i in range(2)] for j in range(3)]
        sm = [[sb(f"sm{j}_{i}", [128, 512], F32) for i in range(2)] for j in range(3)]
        bsm = [[Buf(f"sm{j}_{i}") for i in range(2)] for j in range(3)]
        mm = [sb(f"mm{j}", [128, 512], F32) for j in range(3)]
        bmm = [Buf(f"mm{j}") for j in range(3)]
        wu = [sb(f"wu{i}", [128, 14 * 128], BF16) for i in range(2)]
        bwu = [Buf(f"wu{i}") for i in range(2)]
        bout = Buf("o_x2")
        li = 0
        for ps_ in range(TOK // PASS):
            t0 = ps_ * PASS
            load_x(c, xT, t0)
            for ch in range(6):
                for t in range(NTT):
                    ts = slice(t * 512, (t + 1) * 512)
                    gs = slice(t0 + t * 512, t0 + (t + 1) * 512)
                    i = li % 2
                    li += 1
                    P.dma("sp", lt[0][i][:], hloc[ch * 128:(ch + 1) * 128, gs], blt[0][i])
                    P.dma("sp", lt[1][i][:], pcum[ch * 128:(ch + 1) * 128, gs], blt[1][i])
                    P.dma("sp", lt[2][i][:], pfm32[768 + ch * 128:768 + (ch + 1) * 128, gs], blt[2][i])
                    P.op("dve", lambda e, i=i, ch=ch: e.scalar_tensor_tensor(
                        out=lt[0][i][:], in0=lt[1][i][:], scalar=cin[:, ch:ch + 1], in1=lt[0][i][:],
                        op0=ALU.mult, op1=ALU.add), reads=[blt[1][i], bcin], writes=[blt[0][i]])
                    P.op("dve", lambda e, i=i, ch=ch, ts=ts: e.tensor_tensor(
                        out=c.g[:, 8 + ch, ts], in0=lt[0][i][:], in1=lt[2][i][:], op=ALU.mult),
                        reads=[blt[0][i], blt[2][i]], writes=[c.bg_[8 + ch][t]])
                    P.dma("sp", c.g[:, 14 + ch, ts], obT[ch * 128:(ch + 1) * 128, gs], c.bg_[14 + ch][t])
            for ch in range(2):
                for t in range(NTT):
                    ts = slice(t * 512, (t + 1) * 512)
                    gs = slice(t0 + t * 512, t0 + (t + 1) * 512)
                    P.dma("sp", c.g[:, 20 + ch, ts], odT[ch * 128:(ch + 1) * 128, gs], c.bg_[20 + ch][t])
            for d in range(8):
                w, bw = wu[d % 2], bwu[d % 2]
                P.dma("pool", w[:], wup[d], bw)
                for t in range(NTT):
                    ts = slice(t * 512, (t + 1) * 512)
                    gs = slice(t0 + t * 512, t0 + (t + 1) * 512)
                    i = li % 2
                    li += 1
                    for j in range(3):
                        r0 = 1536 + j * 1024 + d * 128
                        P.dma("sp", sm[j][i][:], pfm32[r0:r0 + 128, gs], bsm[j][i])
                    pa, bpa = next_ps(c)
                    pb, bpb = next_ps(c)
                    pc, bpc = next_ps(c)
                    for k in range(6):
                        P.op("pe", lambda e, w=w, k=k, ts=ts, pa=pa: e.matmul(
                            pa[:], w[:, k * 128:(k + 1) * 128], c.g[:, 8 + k, ts], start=(k == 0), stop=(k == 5)),
                            reads=[bw, c.bg_[8 + k][t]], writes=[bpa])
                    for k in range(6):
                        P.op("pe", lambda e, w=w, k=k, ts=ts, pb=pb: e.matmul(
                            pb[:], w[:, (6 + k) * 128:(7 + k) * 128], c.g[:, 14 + k, ts], start=(k == 0), stop=(k == 5)),
                            reads=[bw, c.bg_[14 + k][t]], writes=[bpb])
                    for k in range(2):
                        P.op("pe", lambda e, w=w, k=k, ts=ts, pc=pc: e.matmul(
                            pc[:], w[:, (12 + k) * 128:(13 + k) * 128], c.g[:, 20 + k, ts], start=(k == 0), stop=(k == 1)),
                            reads=[bw, c.bg_[20 + k][t]], writes=[bpc])
                    for j, (pp, bpp) in enumerate(((pa, bpa), (pb, bpb), (pc, bpc))):
                        P.op("dve", lambda e, j=j, i=i, pp=pp: e.tensor_tensor(
                            out=mm[j][:], in0=sm[j][i][:], in1=pp[:], op=ALU.mult),
                            reads=[bsm[j][i], bpp], writes=[bmm[j]])
                    P.op("pool", lambda e: e.tensor_tensor(out=mm[0][:], in0=mm[0][:], in1=mm[1][:], op=ALU.add),
                         reads=[bmm[1]], writes=[bmm[0]])
                    P.op("pool", lambda e, d=d, ts=ts: e.tensor_tensor(
                        out=c.g[:, d, ts], in0=mm[0][:], in1=mm[2][:], op=ALU.add),
                        reads=[bmm[0], bmm[2]], writes=[c.bg_[d][t]])
            for d in range(8):
                i = c.w13i
                c.w13i = (i + 1) % 3
                w, bw = c.w13[i], c.bw13[i]
                P.dma("pool", w[:, 0:1024], wout[d], bw)
                for t in range(NTT):
                    ts = slice(t * 512, (t + 1) * 512)
                    po, bpo = next_ps(c)
                    for k in range(8):
                        P.op("pe", lambda e, w=w, k=k, ts=ts, po=po: e.matmul(
                            po[:], w[:, k * 128:(k + 1) * 128], c.g[:, k, ts], start=(k == 0), stop=(k == 7)),
                            reads=[bw, c.bg_[k][t]], writes=[bpo])
                    P.op("dve", lambda e, d=d, ts=ts, po=po: e.tensor_tensor(
                        out=c.x[:, d, ts], in0=po[:], in1=c.x[:, d, ts], op=ALU.add),
                        reads=[bpo], writes=[c.bx[d][t]])
            emit_rmsnorm(c, gf_sb, bgf)
            emit_ffn(c, w13, w2)
            if last:
                emit_final_norm(c, gl_sb, bgl)
            store_x(c, x2T, t0, bout)
        P.finish([bout])
        P.emit()
    return nc


TOK = 2048
LRU_C = 8.0


def build_L():
    nc = bass.Bass("TRN2", target_bir_lowering=False)
    dt = lambda name, shape, dtype, kind: nc.dram_tensor(name, shape, dtype, kind=kind).ap()
    axh = dt("axh", [768, 3 + TOK], F32, "ExternalInput")
    cw = dt("cw", [128, 24], F32, "ExternalInput")
    cb = dt("cb", [128, 6], F32, "ExternalInput")
    wa = dt("wa", [128, 6 * 128], F32, "ExternalInput")
    wi = dt("wi", [128, 6 * 128], F32, "ExternalInput")
    ba = dt("ba", [128, 6], F32, "ExternalInput")
    bi = dt("bi", [128, 6], F32, "ExternalInput")
    lam = dt("lam", [128, 6], F32, "ExternalInput")
    hloc = dt("hloc", [768, TOK], F32, "ExternalOutput")
    pcum = dt("pcum", [768, TOK], F32, "ExternalOutput")
    with ExitStack() as es:
        P = Prog(nc, es)
        sb = lambda name, shape, d: es.enter_context(nc.sbuf_tensor(name, shape, d))
        small = {}
        for name, ap, w in (("cw", cw, 24), ("cb", cb, 6), ("ba", ba, 6), ("bi", bi, 6), ("lam", lam, 6)):
            t = sb(name + "_sb", [128, w], F32)
            b = Buf(name)
            P.dma("sp", t[:], ap[:, :], b)
            small[name] = (t, b)
        wa_sb = sb("wa_sb", [128, 768], BF16)
        wi_sb = sb("wi_sb", [128, 768], BF16)
        bwa, bwi = Buf("wa"), Buf("wi")
        P.dma("pool", wa_sb[:], wa[:, :], bwa)
        P.dma("pool", wi_sb[:], wi[:, :], bwi)
        lam_sb, blam = small["lam"]
        nsp = sb("nsp", [128, 6], F32)
        nsp2 = sb("nsp2", [128, 6], F32)
        bnsp, bnsp2 = Buf("nsp"), Buf("nsp2")
        P.op("act", lambda e: e.activation(out=nsp[:], in_=lam_sb[:], func=AF.Exp, scale=-1.0),
             reads=[blam], writes=[bnsp])
        P.op("dve", lambda e: e.tensor_scalar_add(out=nsp[:], in0=nsp[:], scalar1=1.0), reads=[bnsp], writes=[bnsp])
        P.op("act", lambda e: e.activation(out=nsp[:], in_=nsp[:], func=AF.Ln), reads=[bnsp], writes=[bnsp])
        P.op("dve", lambda e: e.tensor_scalar_mul(out=nsp2[:], in0=nsp[:], scalar1=-2.0 * LRU_C),
             reads=[bnsp], writes=[bnsp2])
        P.op("dve", lambda e: e.tensor_scalar_mul(out=nsp[:], in0=nsp[:], scalar1=-LRU_C),
             reads=[bnsp], writes=[bnsp])
        zeros = sb("zeros", [128, TOK], F32)
        bz = Buf("zeros")
        P.op("pool", lambda e: e.memset(zeros[:], 0.0), writes=[bz])
        ax = [sb(f"ax{i}", [128, 3 + TOK], F32) for i in range(2)]
        bax = [Buf(f"ax{i}") for i in range(2)]
        xc = [sb(f"xc{i}", [128, TOK], F32) for i in range(2)]
        bxc = [Buf(f"xc{i}") for i in range(2)]
        xcb = [sb(f"xcb{i}", [128, TOK], BF16) for i in range(2)]
        bxcb = [Buf(f"xcb{i}") for i in range(2)]
        rr = [sb(f"rr{i}", [128, TOK], F32) for i in range(2)]
        brr = [Buf(f"rr{i}") for i in range(2)]
        ii = [sb(f"ii{i}", [128, TOK], F32) for i in range(2)]
        bii = [Buf(f"ii{i}") for i in range(2)]
        aa = [sb(f"aa{i}", [128, TOK], F32) for i in range(2)]
        baa = [Buf(f"aa{i}") for i in range(2)]
        uu = [sb(f"uu{i}", [128, TOK], F32) for i in range(2)]
        buu = [Buf(f"uu{i}") for i in range(2)]
        hh = [sb(f"hh{i}", [128, TOK], F32) for i in range(2)]
        bhh = [Buf(f"hh{i}") for i in range(2)]
        pp_ = [sb(f"pp{i}", [128, TOK], F32) for i in range(2)]
        bpp_ = [Buf(f"pp{i}") for i in range(2)]
        ps = [es.enter_context(nc.psum_tensor(f"ps{i}", [128, 512], F32)) for i in range(4)]
        bps = [Buf(f"ps{i}") for i in range(4)]
        psi = 0
        bo1, bo2 = Buf("o_h"), Buf("o_p")
        cw_sb, bcw = small["cw"]
        cb_sb, bcb = small["cb"]
        ba_sb, bba = small["ba"]
        bi_sb, bbi = small["bi"]
        for ch in range(6):
            i = ch % 2
            P.dma("sp", ax[i][:], axh[ch * 128:(ch + 1) * 128, :], bax[i])
            P.op("dve", lambda e, i=i, ch=ch: e.tensor_scalar(
                out=xc[i][:], in0=ax[i][:, 0:TOK], scalar1=cw_sb[:, ch * 4:ch * 4 + 1], scalar2=cb_sb[:, ch:ch + 1],
                op0=ALU.mult, op1=ALU.add), reads=[bax[i], bcw, bcb], writes=[bxc[i]])
            for j in range(1, 4):
                P.op("dve", lambda e, i=i, ch=ch, j=j: e.scalar_tensor_tensor(
                    out=xc[i][:], in0=ax[i][:, j:j + TOK], scalar=cw_sb[:, ch * 4 + j:ch * 4 + j + 1], in1=xc[i][:],
                    op0=ALU.mult, op1=ALU.add), reads=[bax[i], bcw], writes=[bxc[i]])
            P.op("pool", lambda e, i=i: e.tensor_copy(out=xcb[i][:], in_=xc[i][:]), reads=[bxc[i]], writes=[bxcb[i]])
            for t in range(TOK // 512):
                ts = slice(t * 512, (t + 1) * 512)
                for (w_sb, bw, b_sb, bb, dst, bdst) in ((wa_sb, bwa, ba_sb, bba, rr[i], brr[i]),
                                                         (wi_sb, bwi, bi_sb, bbi, ii[i], bii[i])):
                    p, bp = ps[psi], bps[psi]
                    psi = (psi + 1) % 4
                    P.op("pe", lambda e, w_sb=w_sb, ch=ch, i=i, ts=ts, p=p: e.matmul(
                        p[:], w_sb[:, ch * 128:(ch + 1) * 128], xcb[i][:, ts], start=True, stop=True),
                        reads=[bw, bxcb[i]], writes=[bp])
                    P.op("act", lambda e, dst=dst, ts=ts, p=p, b_sb=b_sb, ch=ch: e.activation(
                        out=dst[:, ts], in_=p[:], func=AF.Sigmoid, bias=b_sb[:, ch:ch + 1]),
                        reads=[bp, bb], writes=[bdst])
            P.op("act", lambda e, i=i, ch=ch: e.activation(out=aa[i][:], in_=rr[i][:], func=AF.Exp,
                                                            scale=nsp[:, ch:ch + 1]),
                 reads=[brr[i], bnsp], writes=[baa[i]])
            P.op("act", lambda e, i=i, ch=ch: e.activation(out=uu[i][:], in_=rr[i][:], func=AF.Exp,
                                                            scale=nsp2[:, ch:ch + 1]),
                 reads=[brr[i], bnsp2], writes=[buu[i]])
            P.op("dve", lambda e, i=i: e.tensor_scalar(out=uu[i][:], in0=uu[i][:], scalar1=-1.0, scalar2=1.0,
                                                        op0=ALU.mult, op1=ALU.add), reads=[buu[i]], writes=[buu[i]])
            P.op("act", lambda e, i=i: e.activation(out=uu[i][:], in_=uu[i][:], func=AF.Sqrt),
                 reads=[buu[i]], writes=[buu[i]])
            P.op("pool", lambda e, i=i: e.tensor_tensor(out=ii[i][:], in0=ii[i][:], in1=xc[i][:], op=ALU.mult),
                 reads=[bxc[i]], writes=[bii[i]])
            P.op("dve", lambda e, i=i: e.tensor_tensor(out=uu[i][:], in0=uu[i][:], in1=ii[i][:], op=ALU.mult),
                 reads=[bii[i]], writes=[buu[i]])
            P.op("dve", lambda e, i=i: e.tensor_tensor_scan(
                out=hh[i][:], data0=aa[i][:], data1=uu[i][:], initial=0.0, op0=ALU.mult, op1=ALU.add),
                reads=[baa[i], buu[i]], writes=[bhh[i]])
            P.op("dve", lambda e, i=i: e.tensor_tensor_scan(
                out=pp_[i][:], data0=aa[i][:], data1=zeros[:], initial=1.0, op0=ALU.mult, op1=ALU.add),
                reads=[baa[i], bz], writes=[bpp_[i]])
            P.dma("sp", hloc[ch * 128:(ch + 1) * 128, :], hh[i][:], bo1, src=bhh[i])
            P.dma("sp", pcum[ch * 128:(ch + 1) * 128, :], pp_[i][:], bo2, src=bpp_[i])
        kcp = dt("kcp", [64, 3 * 128], BF16, "ExternalOutput")
        vcp = dt("vcp", [128, 3 * 64], BF16, "ExternalOutput")
        bo3, bo4 = Buf("o_kc"), Buf("o_vc")
        kcp_sb = sb("kcp_sb", [64, 384], BF16)
        vcp_sb = sb("vcp_sb", [128, 192], BF16)
        bkcp, bvcp = Buf("kcp"), Buf("vcp")
        hid = [sb(f"hid{i}", [128, 256], BF16) for i in range(2)]
        bhid = [Buf(f"hid{i}") for i in range(2)]
        hi = 0
        for kv in range(2):
            k2 = dt(f"k2_{kv}", [128, 3 * 2080], BF16, "ExternalInput")
            w1 = dt(f"cw1_{kv}", [128, 16 * 256], F32, "ExternalInput")
            w2 = dt(f"cw2_{kv}", [128, 2 * 64], F32, "ExternalInput")
            pos2 = dt(f"pos2_{kv}", [128, 16], F32, "ExternalInput")
            k2_sb = sb(f"k2_sb{kv}", [128, 3 * 2080], BF16)
            w1_sb = sb(f"cw1_sb{kv}", [128, 16 * 256], BF16)
            w2_sb = sb(f"cw2_sb{kv}", [128, 128], BF16)
            pos_sb = sb(f"pos_sb{kv}", [128, 16], BF16)
            bias_sb = sb(f"cbias{kv}", [128, 2], F32)
            bk2, bw1, bw2, bpos, bbias = Buf("k2"), Buf("cw1"), Buf("cw2"), Buf("pos"), Buf("cbias")
            P.dma("sp", k2_sb[:], k2[:, :], bk2)
            P.dma("pool", w1_sb[:], w1[:, :], bw1)
            P.dma("pool", w2_sb[:], w2[:, :], bw2)
            P.dma("pool", pos_sb[:], pos2[:, :], bpos)
            for hc in range(2):
                p, bp = ps[psi], bps[psi]
                psi = (psi + 1) % 4
                for lp in range(16):
                    P.op("pe", lambda e, p=p, lp=lp, hc=hc, w1_sb=w1_sb, pos_sb=pos_sb: e.matmul(
                        p[:, 0:1], w1_sb[:, lp * 256 + hc * 128:lp * 256 + (hc + 1) * 128], pos_sb[:, lp:lp + 1],
                        start=(lp == 0), stop=(lp == 15)), reads=[bw1, bpos], writes=[bp])
                P.op("dve", lambda e, p=p, hc=hc, bias_sb=bias_sb: e.tensor_copy(out=bias_sb[:, hc:hc + 1], in_=p[:, 0:1]),
                     reads=[bp], writes=[bbias])
            for kvh in range(3):
                hd, bhd = hid[hi % 2], bhid[hi % 2]
                hi += 1
                for hc in range(2):
                    p, bp = ps[psi], bps[psi]
                    psi = (psi + 1) % 4
                    for lp in range(16):
                        c0 = kvh * 2080 + 2 * lp
                        P.op("pe", lambda e, p=p, lp=lp, hc=hc, c0=c0, w1_sb=w1_sb, k2_sb=k2_sb: e.matmul(
                            p[:, 0:128], w1_sb[:, lp * 256 + hc * 128:lp * 256 + (hc + 1) * 128],
                            k2_sb[:, c0:c0 + 2048:16], start=(lp == 0), stop=(lp == 15)),
                            reads=[bw1, bk2], writes=[bp])
                    P.op("act", lambda e, p=p, hc=hc, hd=hd, bias_sb=bias_sb: e.activation(
                        out=hd[:, hc * 128:(hc + 1) * 128], in_=p[:, 0:128], func=AF.Gelu_apprx_tanh,
                        bias=bias_sb[:, hc:hc + 1]), reads=[bp, bbias], writes=[bhd])
                p, bp = ps[psi], bps[psi]
                psi = (psi + 1) % 4
                if kv == 0:
                    for hc in range(2):
                        P.op("pe", lambda e, p=p, hc=hc, hd=hd, w2_sb=w2_sb: e.matmul(
                            p[0:64, 0:128], w2_sb[:, hc * 64:(hc + 1) * 64], hd[:, hc * 128:(hc + 1) * 128],
                            start=(hc == 0), stop=(hc == 1)), reads=[bw2, bhd], writes=[bp])
                    P.op("dve", lambda e, p=p, kvh=kvh: e.tensor_copy(out=kcp_sb[:, kvh * 128:(kvh + 1) * 128], in_=p[0:64, 0:128]),
                         reads=[bp], writes=[bkcp])
                else:
                    for hc in range(2):
                        P.op("pe", lambda e, p=p, hc=hc, hd=hd, w2_sb=w2_sb: e.matmul(
                            p[:, 0:64], hd[:, hc * 128:(hc + 1) * 128], w2_sb[:, hc * 64:(hc + 1) * 64],
                            start=(hc == 0), stop=(hc == 1)), reads=[bw2, bhd], writes=[bp])
                    P.op("dve", lambda e, p=p, kvh=kvh: e.tensor_copy(out=vcp_sb[:, kvh * 64:(kvh + 1) * 64], in_=p[:, 0:64]),
                         reads=[bp], writes=[bvcp])
        P.dma("sp", kcp[:, :], kcp_sb[:], bo3, src=bkcp)
        P.dma("sp", vcp[:, :], vcp_sb[:], bo4, src=bvcp)
        P.finish([bo1, bo2, bo3, bo4])
        P.emit()
    return nc


def cmp_host_inputs(inp, l, kcT, vcT, core):
    d = {}
    t0 = core * TOK
    for kv, (src, w1n, w2n, pn) in enumerate(((kcT, "cmp_k_w1", "cmp_k_w2", "cmp_pos_k"),
                                              (vcT, "cmp_v_w1", "cmp_v_w2", "cmp_pos_v"))):
        S = src.shape[1]
        seg = np.zeros((192, 2081), NPBF)
        n = min(2081, S - t0)
        seg[:, :n] = src[:, t0:t0 + n]
        a = seg[:, 0:2080].reshape(3, 64, 2080)
        b = seg[:, 1:2081].reshape(3, 64, 2080)
        k2 = np.concatenate([a, b], axis=1)
        d[f"k2_{kv}"] = np.ascontiguousarray(k2.transpose(1, 0, 2).reshape(128, 3 * 2080))
        w1 = inp[w1n][l]
        d[f"cw1_{kv}"] = np.ascontiguousarray(w1.reshape(16, 128, 256).transpose(1, 0, 2).reshape(128, 16 * 256))
        w2 = inp[w2n][l]
        d[f"cw2_{kv}"] = np.ascontiguousarray(w2.reshape(2, 128, 64).transpose(1, 0, 2).reshape(128, 128))
        pos = inp[pn][l].reshape(2048)
        d[f"pos2_{kv}"] = np.ascontiguousarray(pos.reshape(16, 128).T)
    return d


def lru_host_inputs(inp, l):
    pk = lambda v: np.ascontiguousarray(v.reshape(-1, 128).T)
    cwv = inp["conv_w"][l]
    cw = np.ascontiguousarray(cwv.reshape(4, 6, 128).transpose(2, 1, 0).reshape(128, 24))
    wa = np.ascontiguousarray(inp["lru_wa"][l].transpose(1, 0, 2).reshape(128, 768))
    wi = np.ascontiguousarray(inp["lru_wi"][l].transpose(1, 0, 2).reshape(128, 768))
    return {"cw": cw, "cb": pk(inp["conv_b"][l]), "wa": wa, "wi": wi, "ba": pk(inp["lru_ba"][l]),
            "bi": pk(inp["lru_bi"][l]), "lam": pk(inp["lru_lambda"][l])}


TOK = 2048
NT = TOK // 128
DIL = ((128, 1), (512, 4), (2048, 16))
NB = (2, 5, 17)
HALO = (128, 512, 2048)
KLEN = tuple(h + TOK for h in HALO)
NBLK = tuple(k // 128 for k in KLEN)
BIG = 30000.0


def dil_masks():
    kk = np.arange(128)[:, None]
    qq = np.arange(128)[None, :]
    ms = []
    for g, (W, dil) in enumerate(DIL):
        for rr in range(NB[g]):
            dist = (NB[g] - 1 - rr) * 128 + qq - kk
            ok = (dist >= 0) & (dist <= W) & (dist % dil == 0)
            m = np.where(ok, 0.0, -BIG).astype(np.float32)
            ms.append(np.tile(m, (1, 4)))
    return np.ascontiguousarray(np.stack(ms, 1)).astype(NPBF)


def build_D():
    nc = bass.Bass("TRN2", target_bir_lowering=False)
    dt = lambda name, shape, dtype, kind: nc.dram_tensor(name, shape, dtype, kind=kind).ap()
    cq = dt("cq", [64, NT, 12 * 128], BF16, "ExternalInput")
    ck = [dt(f"ck{g}", [64, 4 * KLEN[g]], BF16, "ExternalInput") for g in range(3)]
    cv = [dt(f"cv{g}", [128, NBLK[g] * 260], BF16, "ExternalInput") for g in range(3)]
    db = dt("db", [128, 24 * 512], BF16, "ExternalInput")
    ident = dt("ident", [128, 128], BF16, "ExternalInput")
    odT = dt("odT", [256, TOK], BF16, "ExternalOutput")
    with ExitStack() as es:
        P = Prog(nc, es)
        sb = lambda name, shape, d: es.enter_context(nc.sbuf_tensor(name, shape, d))
        ck_sb = [sb(f"ck_sb{g}", [64, 4 * KLEN[g]], BF16) for g in range(3)]
        bck = [Buf(f"ck{g}") for g in range(3)]
        cv_sb = [sb(f"cv_sb{g}", [128, NBLK[g] * 260], BF16) for g in range(3)]
        bcv = [Buf(f"cv{g}") for g in range(3)]
        for g in range(3):
            for hj in range(4):
                P.dma("sp", ck_sb[g][:, hj * KLEN[g]:(hj + 1) * KLEN[g]], ck[g][:, hj * KLEN[g]:(hj + 1) * KLEN[g]], bck[g])
            P.dma("sp", cv_sb[g][:], cv[g][:, :], bcv[g])
        db_sb = sb("db_sb", [128, 24 * 512], BF16)
        bdb = Buf("db")
        for j in range(4):
            P.dma("sp", db_sb[:, j * 3072:(j + 1) * 3072], db[:, j * 3072:(j + 1) * 3072], bdb)
        id_sb = sb("id_sb", [128, 128], BF16)
        bid = Buf("id")
        P.dma("sp", id_sb[:], ident[:, :], bid)
        cq_sb = [sb(f"cq_sb{i}", [64, 12 * 128], BF16) for i in range(2)]
        bcq = [Buf(f"cq{i}") for i in range(2)]
        pT = [sb(f"pT{i}", [128, 512], BF16) for i in range(3)]
        bpT = [Buf(f"pT{i}") for i in range(3)]
        ps = [es.enter_context(nc.psum_tensor(f"ps{i}", [128, 512], F32)) for i in range(3)]
        bps = [Buf(f"ps{i}") for i in range(3)]
        po = [es.enter_context(nc.psum_tensor(f"po{i}", [128, 260], F32)) for i in range(2)]
        bpo = [Buf(f"po{i}") for i in range(2)]
        ptr = es.enter_context(nc.psum_tensor("ptr", [128, 256], BF16))
        bptr = Buf("ptr")
        rden = sb("rden", [128, 4], F32)
        brden = Buf("rden")
        ob = sb("ob", [128, 256], BF16)
        bob = Buf("ob")
        oT = [sb(f"oT{i}", [128, 256], BF16) for i in range(2)]
        boT = [Buf(f"oT{i}") for i in range(2)]
        bout = Buf("o_od")
        it = 0
        for i in range(NT):
            q_sb, bq = cq_sb[i % 2], bcq[i % 2]
            P.dma("sp", q_sb[:], cq[:, i, :], bq)
            o_ps, bo_ps = po[i % 2], bpo[i % 2]
            mi = 0
            total = sum(NB)
            n = 0
            def emit_scores(g, rr, mi):
                nonlocal it
                kb = i + rr
                p, bp = ps[it % 3], bps[it % 3]
                t_sb, bt = pT[it % 3], bpT[it % 3]
                it += 1
                for hj in range(4):
                    P.op("pe", lambda e, p=p, g=g, hj=hj, kb=kb, q_sb=q_sb: e.matmul(
                        p[:, hj * 128:(hj + 1) * 128],
                        ck_sb[g][:, hj * KLEN[g] + kb * 128:hj * KLEN[g] + (kb + 1) * 128],
                        q_sb[:, (g * 4 + hj) * 128:(g * 4 + hj + 1) * 128], start=(hj == 0), stop=False, skip_group_check=True),
                        reads=[bck[g], bq], writes=[bp])
                P.op("pe", lambda e, p=p, mi=mi: e.matmul(
                    p[:], id_sb[:], db_sb[:, mi * 512:(mi + 1) * 512], start=False, stop=True, skip_group_check=True),
                    reads=[bid, bdb], writes=[bp])
                P.op("act", lambda e, p=p, t_sb=t_sb: e.activation(out=t_sb[:], in_=p[:], func=AF.Exp, scale=0.125),
                     reads=[bp], writes=[bt])
                return (t_sb, bt, g, kb)

            def emit_pv(ctx, n):
                t_sb, bt, g, kb = ctx
                for hj in range(4):
                    P.op("pe", lambda e, o_ps=o_ps, t_sb=t_sb, hj=hj, g=g, kb=kb, n=n: e.matmul(
                        o_ps[:, hj * 65:(hj + 1) * 65], t_sb[:, hj * 128:(hj + 1) * 128],
                        cv_sb[g][:, kb * 260 + hj * 65:kb * 260 + (hj + 1) * 65],
                        start=(n == 0 and hj == 0), stop=(n == total - 1), skip_group_check=True),
                        reads=[bt, bcv[g]], writes=[bo_ps])

            prev = None
            for g in range(3):
                for rr in range(NB[g]):
                    cur = emit_scores(g, rr, mi)
                    if prev is not None:
                        emit_pv(prev, n)
                        n += 1
                    prev = cur
                    mi += 1
            emit_pv(prev, n)
            o3 = o_ps[:].rearrange("p (h e) -> p h e", e=65)
            P.op("dve", lambda e, o3=o3: e.reciprocal(out=rden[:], in_=o3[:, :, 64]), reads=[bo_ps], writes=[brden])
            for hj in range(4):
                P.op("dve", lambda e, o_ps=o_ps, hj=hj: e.tensor_scalar_mul(
                    out=ob[:, hj * 64:(hj + 1) * 64], in0=o_ps[:, hj * 65:hj * 65 + 64], scalar1=rden[:, hj:hj + 1]),
                    reads=[bo_ps, brden], writes=[bob])
            for cc in range(2):
                P.op("pe", lambda e, cc=cc: e.transpose(ptr[:, cc * 128:(cc + 1) * 128], ob[:, cc * 128:(cc + 1) * 128], id_sb[:]),
                     reads=[bob, bid], writes=[bptr])
            o_t, bo_t = oT[i % 2], boT[i % 2]
            P.op("act", lambda e, o_t=o_t: e.activation(out=o_t[:], in_=ptr[:], func=AF.Copy), reads=[bptr], writes=[bo_t])
            for cc in range(2):
                P.dma("sp", odT[cc * 128:(cc + 1) * 128, i * 128:(i + 1) * 128], o_t[:, cc * 128:(cc + 1) * 128],
                      bout, src=bo_t)
        P.finish([bout])
        P.emit()
    return nc


def dil_host_inputs(cqT, ckT, cv, core):
    S = cqT.shape[1]
    t0 = core * TOK
    d = {}
    q = cqT[:, t0:t0 + TOK].reshape(12, 64, NT, 128)
    d["cq"] = np.ascontiguousarray(q.transpose(1, 2, 0, 3).reshape(64, NT, 12 * 128))
    for g in range(3):
        h = HALO[g]
        lo = t0 - h
        kseg = np.zeros((768, KLEN[g]), NPBF)
        vseg = np.zeros((KLEN[g], 768), NPBF)
        valid = np.zeros((KLEN[g],), NPBF)
        s = max(lo, 0)
        kseg[:, s - lo:] = ckT[:, s:t0 + TOK]
        vseg[s - lo:] = cv[s:t0 + TOK]
        valid[s - lo:] = 1
        kk = kseg[g * 256:(g + 1) * 256].reshape(4, 64, KLEN[g])
        d[f"ck{g}"] = np.ascontiguousarray(kk.transpose(1, 0, 2).reshape(64, 4 * KLEN[g]))
        vv = vseg[:, g * 256:(g + 1) * 256].reshape(NBLK[g], 128, 4, 64)
        va = np.zeros((NBLK[g], 128, 4, 65), NPBF)
        va[..., :64] = vv
        va[..., 64] = valid.reshape(NBLK[g], 128)[:, :, None]
        d[f"cv{g}"] = np.ascontiguousarray(va.transpose(1, 0, 2, 3).reshape(128, NBLK[g] * 260))
    return d


TOK = 2048
NT = TOK // 128
S = 16384
NKB = S // 128
BIG = 30000.0


def nsa_consts(core):
    t = core * TOK + np.arange(TOK)
    c = np.arange(1024)
    cmask = ((16 * c[None, :] + 31) <= t[:, None]) & (c[None, :] < 1023)
    n = np.arange(256)
    cur = (t // 64)[:, None]
    forced = (n[None, :] == cur) | (n[None, :] == 0)
    fut = (n[None, :] > cur) & ~forced
    keep = ~(forced | fut)
    add = 1e9 * forced - 1.0 * fut
    gt = (t // 128)[:, None]
    past = n[None, :] < 2 * gt
    cf = np.stack([keep.astype(np.float32), add.astype(np.float32), past.astype(np.float32)], 1)
    constF = np.ascontiguousarray(cf.reshape(NT, 128, 768).transpose(1, 0, 2).reshape(128, NT * 768))
    kk = np.arange(128)
    ed = np.zeros((NT, 2, 128, 128), np.float32)
    for i in range(NT):
        g = core * NT + i
        for k in range(128):
            nb = 2 * g + k // 64
            ed[i, nb // 128, nb % 128, k] = 1.0
    cb = np.concatenate([cmask.reshape(NT, 128, 1024).astype(np.float32),
                         ed[:, 0].transpose(0, 1, 2), ed[:, 1]], axis=2)
    constB = np.ascontiguousarray(cb.transpose(1, 0, 2).reshape(128, NT * 1280)).astype(NPBF)
    return constF, constB


def nsa_static():
    kk = np.arange(128)[:, None]
    qq = np.arange(128)[None, :]
    trib = np.tile(np.where(kk > qq, -BIG, 0.0), (1, 4)).astype(np.float32)
    wb0 = np.tile(np.where(kk > qq, 0.0, -BIG), (1, 4)).astype(np.float32)
    eexp = np.zeros((128, 64, 128), np.float32)
    for jj in range(64):
        for k in range(128):
            eexp[2 * jj + k // 64, jj, k] = 1.0
    ident = np.eye(128, dtype=np.float32)
    st = np.concatenate([trib, wb0, ident, eexp.reshape(128, 64 * 128)], axis=1)
    return np.ascontiguousarray(st).astype(NPBF)


ST_W = 512 + 512 + 128 + 8192


def build_N(kvh_list=(0, 1, 2), tile_list=tuple(range(NT)), n_main=NKB, skip=()):
    nc = bass.Bass("TRN2", target_bir_lowering=False)
    dt = lambda name, shape, dtype, kind: nc.dram_tensor(name, shape, dtype, kind=kind).ap()
    qr = dt("qr", [64, NT * 3 * 512], BF16, "ExternalInput")
    kc = dt("kc", [64, 3 * 1024], BF16, "ExternalInput")
    vc = dt("vc", [128, 3 * 8 * 65], BF16, "ExternalInput")
    ks = dt("ks", [64, 3 * S], BF16, "ExternalInput")
    vs = dt("vs", [128, 3 * NKB * 65], BF16, "ExternalInput")
    kso = dt("kso", [64, 3 * TOK], BF16, "ExternalInput")
    vso = dt("vso", [128, 3 * NT * 65], BF16, "ExternalInput")
    kw = dt("kw", [64, 3 * 2560], BF16, "ExternalInput")
    vw = dt("vw", [128, 3 * 20 * 65], BF16, "ExternalInput")
    gate = dt("gate", [128, NT * 36], F32, "ExternalInput")
    constF = dt("constF", [128, NT * 768], F32, "ExternalInput")
    constB = dt("constB", [128, NT * 1280], BF16, "ExternalInput")
    stat = dt("stat", [128, ST_W], BF16, "ExternalInput")
    obT = dt("obT", [768, TOK], BF16, "ExternalOutput")
    with ExitStack() as es:
        P = Prog(nc, es)
        sb = lambda name, shape, d: es.enter_context(nc.sbuf_tensor(name, shape, d))
        pst = lambda name, shape, d: es.enter_context(nc.psum_tensor(name, shape, d))
        stat_sb = sb("stat_sb", [128, ST_W], BF16)
        bstat = Buf("stat")
        for j in range(4):
            w = ST_W // 4
            P.dma("sp", stat_sb[:, j * w:(j + 1) * w], stat[:, j * w:(j + 1) * w], bstat)
        trib = stat_sb[:, 0:512]
        wb0 = stat_sb[:, 512:1024]
        ident = stat_sb[:, 1024:1152]
        eexp = lambda jj: stat_sb[:, 1152 + jj * 128:1152 + (jj + 1) * 128]
        gate_sb = sb("gate_sb", [128, NT * 36], F32)
        bgate = Buf("gate")
        P.dma("sp", gate_sb[:], gate[:, :], bgate)
        ks_sb = sb("ks_sb", [64, S], BF16)
        vs_sb = sb("vs_sb", [128, NKB * 65], BF16)
        kso_sb = sb("kso_sb", [64, TOK], BF16)
        vso_sb = sb("vso_sb", [128, NT * 65], BF16)
        kw_sb = sb("kw_sb", [64, 2560], BF16)
        vw_sb = sb("vw_sb", [128, 20 * 65], BF16)
        kc_sb = sb("kc_sb", [64, 1024], BF16)
        vc_sb = sb("vc_sb", [128, 8 * 65], BF16)
        bks, bvs, bkso, bvso, bkw, bvw, bkc, bvc = [Buf(n) for n in ("ks", "vs", "kso", "vso", "kw", "vw", "kc", "vc")]
        q_sb = [sb(f"q_sb{i}", [64, 512], BF16) for i in range(2)]
        bq = [Buf(f"q{i}") for i in range(2)]
        cF = [sb(f"cF{i}", [128, 768], F32) for i in range(2)]
        bcF = [Buf(f"cF{i}") for i in range(2)]
        cB = [sb(f"cB{i}", [128, 1280], BF16) for i in range(2)]
        bcB = [Buf(f"cB{i}") for i in range(2)]
        e_sb = [sb(f"e_sb{i}", [128, 1024], F32) for i in range(2)]
        be = [Buf(f"e{i}") for i in range(2)]
        em16 = [sb(f"em16_{i}", [128, 1024], BF16) for i in range(2)]
        bem16 = [Buf(f"em16_{i}") for i in range(2)]
        PT = [sb(f"PT{i}", [128, 1024], BF16) for i in range(2)]
        bPT = [Buf(f"PT{i}") for i in range(2)]
        imp = sb("imp", [128, 1024], F32)
        bimp = Buf("imp")
        sm = sb("sm", [128, 8], F32)
        bsm = [Buf(f"sm{i}") for i in range(4)]
        imps = sb("imps", [128, 256], F32)
        impm = sb("impm", [128, 256], F32)
        imp2 = sb("imp2", [128, 256], F32)
        sel = sb("sel", [128, 256], F32)
        bimps, bimpm, bimp2, bsel = Buf("imps"), Buf("impm"), Buf("imp2"), Buf("sel")
        m8 = sb("m8", [128, 16], F32)
        bm8 = Buf("m8")
        selb = sb("selb", [128, 512], BF16)
        bselb = Buf("selb")
        sbr = [sb(f"sbr{i}", [128, 512], BF16) for i in range(4)]
        bsbr = [Buf(f"sbr{i}") for i in range(4)]
        pT = [sb(f"pT{i}", [128, 512], BF16) for i in range(3)]
        bpT = [Buf(f"pT{i}") for i in range(3)]
        cf = sb("cf", [128, 16], F32)
        bcf = Buf("cf")
        of32 = sb("of32", [128, 256], F32)
        bof = Buf("of32")
        ob16 = sb("ob16", [128, 256], BF16)
        bob = Buf("ob16")
        oT = [sb(f"oT{i}", [128, 256], BF16) for i in range(2)]
        boT = [Buf(f"oT{i}") for i in range(2)]
        psc = pst("psc", [128, 1024], F32)
        bpsc = Buf("psc")
        ptr = pst("ptr", [128, 1024], BF16)
        bptr = Buf("ptr")
        pO = [pst(f"pO{i}", [128, 512], F32) for i in range(3)]
        bpO = [Buf(f"pO{i}") for i in range(3)]
        ps = [pst(f"ps{i}", [128, 512], F32) for i in range(2)]
        bps = [Buf(f"ps{i}") for i in range(2)]
        bout = Buf("o_ob")
        it = 0
        ci = 0
        for kvh in kvh_list:
            for j in range(4):
                w = S // 4
                P.dma("sp", ks_sb[:, j * w:(j + 1) * w], ks[:, kvh * S + j * w:kvh * S + (j + 1) * w], bks)
                w = NKB * 65 // 4
                P.dma("sp", vs_sb[:, j * w:(j + 1) * w], vs[:, kvh * NKB * 65 + j * w:kvh * NKB * 65 + (j + 1) * w], bvs)
            P.dma("sp", kso_sb[:], kso[:, kvh * TOK:(kvh + 1) * TOK], bkso)
            P.dma("sp", vso_sb[:], vso[:, kvh * NT * 65:(kvh + 1) * NT * 65], bvso)
            P.dma("sp", kw_sb[:], kw[:, kvh * 2560:(kvh + 1) * 2560], bkw)
            P.dma("sp", vw_sb[:], vw[:, kvh * 1300:(kvh + 1) * 1300], bvw)
            P.dma("sp", kc_sb[:], kc[:, kvh * 1024:(kvh + 1) * 1024], bkc)
            P.dma("sp", vc_sb[:], vc[:, kvh * 520:(kvh + 1) * 520], bvc)
            for i in tile_list:
                b2 = (kvh * NT + i) % 2
                q, bq_ = q_sb[b2], bq[b2]
                P.dma("sp", q[:], qr[:, (i * 3 + kvh) * 512:(i * 3 + kvh + 1) * 512], bq_)
                cF_, bcF_ = cF[b2], bcF[b2]
                cB_, bcB_ = cB[b2], bcB[b2]
                P.dma("sp", cF_[:], constF[:, i * 768:(i + 1) * 768], bcF_)
                P.dma("sp", cB_[:], constB[:, i * 1280:(i + 1) * 1280], bcB_)
                Oc, Os, Ow = pO
                bOc, bOs, bOw = bpO
                for g in (range(4) if 'cmp' not in skip else ()):
                    for hh in range(2):
                        P.op("pe", lambda e, q=q, g=g, hh=hh: e.matmul(
                            psc[:, hh * 512:(hh + 1) * 512], q[:, g * 128:(g + 1) * 128],
                            kc_sb[:, hh * 512:(hh + 1) * 512], start=True, stop=True),
                            reads=[bq_, bkc], writes=[bpsc])
                    P.op("dve", lambda e: e.reduce_max(out=sm[:, 0:1], in_=psc[:], axis=AX.X),
                         reads=[bpsc], writes=[bsm[0]])
                    P.op("dve", lambda e: e.tensor_scalar_mul(out=sm[:, 1:2], in0=sm[:, 0:1], scalar1=-0.125),
                         reads=[bsm[0]], writes=[bsm[1]])
                    ee, bee = e_sb[ci % 2], be[ci % 2]
                    e16, be16 = em16[ci % 2], bem16[ci % 2]
                    PT_, bPT_ = PT[ci % 2], bPT[ci % 2]
                    ci += 1
                    P.op("act", lambda e, ee=ee: e.activation(out=ee[:], in_=psc[:], func=AF.Exp, bias=sm[:, 1:2], scale=0.125),
                         reads=[bpsc, bsm[1]], writes=[bee])
                    P.op("dve", lambda e, ee=ee, cB_=cB_: e.tensor_tensor(out=ee[:], in0=ee[:], in1=cB_[:, 0:1024], op=ALU.mult),
                         reads=[bcB_], writes=[bee])
                    P.op("dve", lambda e, ee=ee: e.reduce_sum(out=sm[:, 2:3], in_=ee[:], axis=AX.X),
                         reads=[bee], writes=[bsm[2]])
                    P.op("dve", lambda e: e.tensor_scalar_max(out=sm[:, 2:3], in0=sm[:, 2:3], scalar1=1e-30),
                         reads=[bsm[2]], writes=[bsm[2]])
                    P.op("dve", lambda e: e.reciprocal(out=sm[:, 3:4], in_=sm[:, 2:3]), reads=[bsm[2]], writes=[bsm[3]])
                    if g == 0:
                        P.op("dve", lambda e, ee=ee: e.tensor_scalar_mul(out=imp[:], in0=ee[:], scalar1=sm[:, 3:4]),
                             reads=[bee, bsm[3]], writes=[bimp])
                    else:
                        P.op("dve", lambda e, ee=ee: e.scalar_tensor_tensor(
                            out=imp[:], in0=ee[:], scalar=sm[:, 3:4], in1=imp[:], op0=ALU.mult, op1=ALU.add),
                            reads=[bee, bsm[3]], writes=[bimp])
                    P.op("dve", lambda e, ee=ee, e16=e16: e.tensor_copy(out=e16[:], in_=ee[:]), reads=[bee], writes=[be16])
                    for cbk in range(8):
                        P.op("pe", lambda e, e16=e16, cbk=cbk: e.transpose(
                            ptr[:, cbk * 128:(cbk + 1) * 128], e16[:, cbk * 128:(cbk + 1) * 128], ident),
                            reads=[be16, bstat], writes=[bptr])
                    P.op("act", lambda e, PT_=PT_: e.activation(out=PT_[:], in_=ptr[:], func=AF.Copy),
                         reads=[bptr], writes=[bPT_])
                    for cbk in range(8):
                        P.op("pe", lambda e, PT_=PT_, cbk=cbk, g=g: e.matmul(
                            Oc[:, g * 65:(g + 1) * 65], PT_[:, cbk * 128:(cbk + 1) * 128],
                            vc_sb[:, cbk * 65:(cbk + 1) * 65], start=(g == 0 and cbk == 0), stop=(cbk == 7),
                            skip_group_check=True), reads=[bPT_, bvc], writes=[bOc])
                P.op("dve", lambda e: e.tensor_reduce(out=imps[:], in_=imp[:].rearrange("p (n f) -> p n f", f=4),
                                                       axis=AX.X, op=ALU.add), reads=[bimp], writes=[bimps])
                P.op("dve", lambda e, cF_=cF_: e.tensor_tensor(out=impm[:], in0=imps[:], in1=cF_[:, 0:256], op=ALU.mult),
                     reads=[bimps, bcF_], writes=[bimpm])
                P.op("dve", lambda e, cF_=cF_: e.tensor_tensor(out=impm[:], in0=impm[:], in1=cF_[:, 256:512], op=ALU.add),
                     reads=[bcF_], writes=[bimpm])
                P.op("dve", lambda e: e.max(out=m8[:, 0:8], in_=impm[:]), reads=[bimpm], writes=[bm8])
                P.op("dve", lambda e: e.match_replace(out=imp2[:], in_to_replace=m8[:, 0:8], in_values=impm[:], imm_value=-2.0),
                     reads=[bimpm, bm8], writes=[bimp2])
                P.op("dve", lambda e: e.max(out=m8[:, 8:16], in_=imp2[:]), reads=[bimp2], writes=[bm8])
                P.op("dve", lambda e: e.tensor_scalar(out=sel[:], in0=impm[:], scalar1=m8[:, 15:16], scalar2=None, op0=ALU.is_ge),
                     reads=[bimpm, bm8], writes=[bsel])
                P.op("dve", lambda e: e.tensor_scalar(out=selb[:, 256:512], in0=sel[:], scalar1=-1.0, scalar2=BIG,
                                                       op0=ALU.add, op1=ALU.mult), reads=[bsel], writes=[bselb])
                P.op("dve", lambda e, cF_=cF_: e.tensor_tensor(out=sel[:], in0=sel[:], in1=cF_[:, 512:768], op=ALU.mult),
                     reads=[bcF_], writes=[bsel])
                P.op("dve", lambda e: e.tensor_scalar(out=selb[:, 0:256], in0=sel[:], scalar1=-1.0, scalar2=BIG,
                                                       op0=ALU.add, op1=ALU.mult), reads=[bsel], writes=[bselb])
                for k4 in range(4):
                    P.op("pe", lambda e, k4=k4: e.transpose(ptr[:, k4 * 128:(k4 + 1) * 128], selb[:, k4 * 128:(k4 + 1) * 128], ident),
                         reads=[bselb, bstat], writes=[bptr])
                for k4 in range(4):
                    for g in range(4):
                        eng = "act" if (g % 2 == 0) else "dve"
                        if eng == "act":
                            P.op("act", lambda e, k4=k4, g=g: e.activation(
                                out=sbr[k4][:, g * 128:(g + 1) * 128], in_=ptr[:, k4 * 128:(k4 + 1) * 128], func=AF.Copy),
                                reads=[bptr], writes=[bsbr[k4]])
                        else:
                            P.op("dve", lambda e, k4=k4, g=g: e.tensor_copy(
                                out=sbr[k4][:, g * 128:(g + 1) * 128], in_=ptr[:, k4 * 128:(k4 + 1) * 128]),
                                reads=[bptr], writes=[bsbr[k4]])
                blocks = [("s", j) for j in list(range(n_main)) + [NKB]]
                if 'win' not in skip:
                    blocks += [("w", r) for r in range(5)]
                first_s = blocks[0][1]

                def emit_scores(kind, j):
                    nonlocal it
                    p, bp = ps[it % 2], bps[it % 2]
                    t_sb, bt = pT[it % 3], bpT[it % 3]
                    it += 1
                    if kind == "s" and j < NKB:
                        P.op("pe", lambda e, p=p, j=j, q=q: e.matmul(
                            p[:], ks_sb[:, j * 128:(j + 1) * 128], q[:], start=True, stop=False, skip_group_check=True),
                            reads=[bks, bq_], writes=[bp])
                        P.op("pe", lambda e, p=p, j=j: e.matmul(
                            p[:], eexp(j % 64), sbr[j // 64][:], start=False, stop=True, skip_group_check=True),
                            reads=[bstat, bsbr[j // 64]], writes=[bp])
                        vsl, bv = vs_sb[:, j * 65:(j + 1) * 65], bvs
                        O_, bO_, st, sp_ = Os, bOs, (j == first_s), False
                    elif kind == "s":
                        P.op("pe", lambda e, p=p, i=i, q=q: e.matmul(
                            p[:], kso_sb[:, i * 128:(i + 1) * 128], q[:], start=True, stop=False, skip_group_check=True),
                            reads=[bkso, bq_], writes=[bp])
                        P.op("pe", lambda e, p=p, cB_=cB_: e.matmul(
                            p[:], cB_[:, 1024:1152], sbr[2][:], start=False, stop=False, skip_group_check=True),
                            reads=[bcB_, bsbr[2]], writes=[bp])
                        P.op("pe", lambda e, p=p, cB_=cB_: e.matmul(
                            p[:], cB_[:, 1152:1280], sbr[3][:], start=False, stop=False, skip_group_check=True),
                            reads=[bcB_, bsbr[3]], writes=[bp])
                        P.op("pe", lambda e, p=p: e.matmul(
                            p[:], ident, trib, start=False, stop=True, skip_group_check=True),
                            reads=[bstat], writes=[bp])
                        vsl, bv = vso_sb[:, i * 65:(i + 1) * 65], bvso
                        O_, bO_, st, sp_ = Os, bOs, (j == first_s), True
                    else:
                        r = j
                        kb = i + r
                        last = r not in (0, 4)
                        P.op("pe", lambda e, p=p, kb=kb, q=q, last=last: e.matmul(
                            p[:], kw_sb[:, kb * 128:(kb + 1) * 128], q[:], start=True, stop=last, skip_group_check=True),
                            reads=[bkw, bq_], writes=[bp])
                        if r == 0:
                            P.op("pe", lambda e, p=p: e.matmul(p[:], ident, wb0, start=False, stop=True, skip_group_check=True),
                                 reads=[bstat], writes=[bp])
                        if r == 4:
                            P.op("pe", lambda e, p=p: e.matmul(p[:], ident, trib, start=False, stop=True, skip_group_check=True),
                                 reads=[bstat], writes=[bp])
                        vsl, bv = vw_sb[:, kb * 65:(kb + 1) * 65], bvw
                        O_, bO_, st, sp_ = Ow, bOw, (r == 0), (r == 4)
                    P.op("act", lambda e, p=p, t_sb=t_sb: e.activation(out=t_sb[:], in_=p[:], func=AF.Exp, scale=0.125),
                         reads=[bp], writes=[bt])
                    return (t_sb, bt, vsl, bv, O_, bO_, st, sp_)

                def emit_pv(ctx):
                    t_sb, bt, vsl, bv, O_, bO_, st, sp_ = ctx
                    for g in range(4):
                        P.op("pe", lambda e, t_sb=t_sb, g=g, vsl=vsl, O_=O_, st=st, sp_=sp_: e.matmul(
                            O_[:, g * 65:(g + 1) * 65], t_sb[:, g * 128:(g + 1) * 128], vsl,
                            start=(st and g == 0), stop=sp_, skip_group_check=True),
                            reads=[bt, bv], writes=[bO_])

                prev = None
                for kind, j in blocks:
                    cur = emit_scores(kind, j)
                    if prev is not None:
                        emit_pv(prev)
                    prev = cur
                emit_pv(prev)
                for b, (O_, bO_) in enumerate(((Oc, bOc), (Os, bOs), (Ow, bOw))):
                    o3 = O_[:, 0:260].rearrange("p (h e) -> p h e", e=65)
                    P.op("dve", lambda e, o3=o3: e.tensor_scalar_max(out=cf[:, 0:4], in0=o3[:, :, 64], scalar1=1e-30),
                         reads=[bO_], writes=[bcf])
                    P.op("dve", lambda e: e.reciprocal(out=cf[:, 0:4], in_=cf[:, 0:4]), reads=[bcf], writes=[bcf])
                    g0 = i * 36 + kvh * 12 + b
                    gsl = gate_sb[:, g0:g0 + 10:3]
                    P.op("dve", lambda e, b=b, gsl=gsl: e.tensor_tensor(
                        out=cf[:, 4 + 4 * b:8 + 4 * b], in0=cf[:, 0:4], in1=gsl, op=ALU.mult),
                        reads=[bcf, bgate], writes=[bcf])
                    for g in range(4):
                        dst = ob16 if b == 2 else of32
                        bdst = bob if b == 2 else bof
                        if b == 0:
                            P.op("dve", lambda e, O_=O_, g=g, b=b: e.tensor_scalar_mul(
                                out=of32[:, g * 64:(g + 1) * 64], in0=O_[:, g * 65:g * 65 + 64],
                                scalar1=cf[:, 4 + 4 * b + g:5 + 4 * b + g]), reads=[bO_, bcf], writes=[bof])
                        else:
                            P.op("dve", lambda e, O_=O_, g=g, b=b, dst=dst: e.scalar_tensor_tensor(
                                out=dst[:, g * 64:(g + 1) * 64], in0=O_[:, g * 65:g * 65 + 64],
                                scalar=cf[:, 4 + 4 * b + g:5 + 4 * b + g], in1=of32[:, g * 64:(g + 1) * 64],
                                op0=ALU.mult, op1=ALU.add), reads=[bO_, bcf, bof], writes=[bdst])
                for cc in range(2):
                    P.op("pe", lambda e, cc=cc: e.transpose(ptr[:, cc * 128:(cc + 1) * 128], ob16[:, cc * 128:(cc + 1) * 128], ident),
                         reads=[bob, bstat], writes=[bptr])
                o_t, bo_t = oT[b2], boT[b2]
                P.op("act", lambda e, o_t=o_t: e.activation(out=o_t[:], in_=ptr[:, 0:256], func=AF.Copy),
                     reads=[bptr], writes=[bo_t])
                for cc in range(2):
                    r0 = kvh * 256 + cc * 128
                    P.dma("sp", obT[r0:r0 + 128, i * 128:(i + 1) * 128], o_t[:, cc * 128:(cc + 1) * 128], bout, src=bo_t)
        P.finish([bout])
        P.emit()
    return nc


def nsa_host_inputs(qT, kcT, vcm, ksT, vsm, kwT, vwm, gate, core):
    t0 = core * TOK
    d = {}
    q = qT[:, t0:t0 + TOK].reshape(3, 4, 64, NT, 128)
    d["qr"] = np.ascontiguousarray(q.transpose(2, 3, 0, 1, 4).reshape(64, NT * 3 * 512))
    d["kc"] = np.ascontiguousarray(kcT.transpose(1, 0, 2).reshape(64, 3 * 1024))
    va = np.ones((1024, 3, 65), NPBF)
    va[:, :, :64] = vcm
    d["vc"] = np.ascontiguousarray(va.reshape(8, 128, 3, 65).transpose(1, 2, 0, 3).reshape(128, 3 * 8 * 65))
    d["ks"] = np.ascontiguousarray(ksT.reshape(3, 64, S).transpose(1, 0, 2).reshape(64, 3 * S))
    va = np.ones((S, 3, 65), NPBF)
    va[:, :, :64] = vsm.reshape(S, 3, 64)
    d["vs"] = np.ascontiguousarray(va.reshape(NKB, 128, 3, 65).transpose(1, 2, 0, 3).reshape(128, 3 * NKB * 65))
    d["kso"] = np.ascontiguousarray(ksT[:, t0:t0 + TOK].reshape(3, 64, TOK).transpose(1, 0, 2).reshape(64, 3 * TOK))
    d["vso"] = np.ascontiguousarray(
        va[t0:t0 + TOK].reshape(NT, 128, 3, 65).transpose(1, 2, 0, 3).reshape(128, 3 * NT * 65))
    kseg = np.zeros((192, 2560), NPBF)
    vseg = np.zeros((2560, 3, 65), NPBF)
    lo = t0 - 512
    s = max(lo, 0)
    kseg[:, s - lo:] = kwT[:, s:t0 + TOK]
    vseg[s - lo:, :, :64] = vwm[s:t0 + TOK].reshape(-1, 3, 64)
    vseg[s - lo:, :, 64] = 1
    d["kw"] = np.ascontiguousarray(kseg.reshape(3, 64, 2560).transpose(1, 0, 2).reshape(64, 3 * 2560))
    d["vw"] = np.ascontiguousarray(vseg.reshape(20, 128, 3, 65).transpose(1, 2, 0, 3).reshape(128, 3 * 20 * 65))
    d["gate"] = np.ascontiguousarray(gate[t0:t0 + TOK].reshape(NT, 128, 36).transpose(1, 0, 2).reshape(128, NT * 36))
    return d


NCORES = 8
N_CHUNK = 4
_NC_CACHE = {}
_DEBUG = {}


def _get_nc(name):
    if name == "A":
        return build_A()
    if name == "L":
        return build_L()
    if name == "D":
        return build_D()
    if isinstance(name, tuple) and name[0] == "N":
        return build_N()
    if name == "C0":
        return build_C(False)
    if name == "C1":
        return build_C(True)
    raise KeyError(name)


def _run(name, in_maps):
    nc = _get_nc(name)
    res = run_bass_kernel_spmd(nc, in_maps, core_ids=list(range(NCORES)))
    return res.results


def kernel(x, ffn1_norm, ffn1_w1, ffn1_w3, ffn1_w2, mix_norm, w_in, conv_w, conv_b,
           lru_wa, lru_ba, lru_wi, lru_bi, lru_lambda, cmp_pos_k, cmp_pos_v,
           cmp_k_w1, cmp_k_w2, cmp_v_w1, cmp_v_w2, w_up_a, w_up_b, w_up_c, w_out,
           ffn2_norm, ffn2_w1, ffn2_w3, ffn2_w2, final_norm, _layers=None, _debug=None):
    f32 = lambda a: np.ascontiguousarray(np.asarray(a, dtype=np.float32))
    inp = {k: f32(v) for k, v in dict(
        conv_w=conv_w, conv_b=conv_b, lru_wa=lru_wa, lru_ba=lru_ba, lru_wi=lru_wi, lru_bi=lru_bi,
        lru_lambda=lru_lambda, cmp_pos_k=cmp_pos_k, cmp_pos_v=cmp_pos_v, cmp_k_w1=cmp_k_w1,
        cmp_k_w2=cmp_k_w2, cmp_v_w1=cmp_v_w1, cmp_v_w2=cmp_v_w2).items()}
    x = f32(x)[0]
    Sx = x.shape[0]
    depth = ffn1_norm.shape[0]
    layers = range(depth) if _layers is None else _layers
    xT = [np.ascontiguousarray(x[c * TOK:(c + 1) * TOK].T) for c in range(NCORES)]
    db = dil_masks().reshape(128, 24 * 512)
    identb = np.eye(128, dtype=np.float32).astype(NPBF)
    stat = nsa_static()
    nconst = [nsa_consts(c) for c in range(NCORES)]
    gl = vec_pk(f32(final_norm))
    for l in layers:
        w13, w2 = ffn_layout(f32(ffn1_w1[l]), f32(ffn1_w3[l]), f32(ffn1_w2[l]))
        wfm, wtm = w_in_layout(f32(w_in[l]))
        gf, gm = vec_pk(f32(ffn1_norm[l])), vec_pk(f32(mix_norm[l]))
        rA = _run("A", [{"xT": xT[c], "w13": w13, "w2": w2, "gf": gf, "gm": gm, "wfm": wfm, "wtm": wtm}
                        for c in range(NCORES)])
        del w13, w2, wfm, wtm
        fm16 = np.concatenate([rA[c]["pfm16"] for c in range(NCORES)], axis=1)
        tm16 = np.concatenate([rA[c]["ptm16"] for c in range(NCORES)], axis=0)
        gate = np.concatenate([rA[c]["pgate"] for c in range(NCORES)], axis=0)
        ax = np.concatenate([rA[c]["pfm32"][0:768] for c in range(NCORES)], axis=1)
        axp = np.concatenate([np.zeros((768, 3), np.float32), ax], axis=1)
        qT, kcT, vcT = fm16[0:768], fm16[768:960], fm16[960:1152]
        ksT, kwT, cqT, ckT = fm16[1152:1344], fm16[1344:1536], fm16[1536:2304], fm16[2304:3072]
        vsm, vwm, cvm = tm16[:, 0:192], tm16[:, 192:384], tm16[:, 384:1152]
        lp = lru_host_inputs(inp, l)
        mapsL = []
        for c in range(NCORES):
            d = dict(lp)
            d["axh"] = np.ascontiguousarray(axp[:, c * TOK:c * TOK + TOK + 3])
            d.update(cmp_host_inputs(inp, l, kcT, vcT, c))
            mapsL.append(d)
        rL = _run("L", mapsL)
        del mapsL
        kcc = np.stack([rL[c]["kcp"].reshape(64, 3, 128) for c in range(NCORES)], 0)
        kcfull = np.ascontiguousarray(kcc.transpose(2, 1, 0, 3).reshape(3, 64, 1024))
        vcfull = np.concatenate([rL[c]["vcp"].reshape(128, 3, 64) for c in range(NCORES)], 0)
        ends = np.zeros((128, 6, 2, NCORES), np.float32)
        for c in range(NCORES):
            ends[:, :, 0, c] = rL[c]["pcum"][:, -1].reshape(6, 128).T
            ends[:, :, 1, c] = rL[c]["hloc"][:, -1].reshape(6, 128).T
        ends = np.ascontiguousarray(ends.reshape(128, 96))
        mapsD = []
        for c in range(NCORES):
            d = dil_host_inputs(cqT, ckT, cvm, c)
            d["db"] = db
            d["ident"] = identb
            mapsD.append(d)
        rD = _run("D", mapsD)
        del mapsD
        mapsN = []
        for c in range(NCORES):
            d = nsa_host_inputs(qT, kcfull, vcfull, ksT, vsm, kwT, vwm, gate, c)
            d["constF"], d["constB"] = nconst[c]
            d["stat"] = stat
            mapsN.append(d)
        rN = _run(("N",), mapsN)
        del mapsN
        w13, w2 = ffn_layout(f32(ffn2_w1[l]), f32(ffn2_w3[l]), f32(ffn2_w2[l]))
        wup = chunked(np.concatenate([f32(w_up_a[l]), f32(w_up_b[l]), f32(w_up_c[l])], axis=0))
        wo = chunked(f32(w_out[l]))
        gf2 = vec_pk(f32(ffn2_norm[l]))
        last = (l == depth - 1)
        mapsC = []
        for c in range(NCORES):
            cs = np.zeros((128, 8), np.float32)
            if c > 0:
                cs[:, c - 1] = 1.0
            mapsC.append({"xT": rA[c]["x1T"], "pfm32": rA[c]["pfm32"], "hloc": rL[c]["hloc"], "pcum": rL[c]["pcum"],
                          "ends": ends, "csel": cs, "obT": rN[c]["obT"], "odT": rD[c]["odT"], "wup": wup,
                          "wout": wo, "w13": w13, "w2": w2, "gf": gf2, "gl": gl})
        rC = _run("C1" if last else "C0", mapsC)
        if _debug is not None:
            _debug[l] = dict(rA=rA, rL=rL, rD=rD, rN=rN, rC=rC, kcfull=kcfull, vcfull=vcfull, fm16=fm16, tm16=tm16,
                             gate=gate, ax=ax)
        xT = [rC[c]["x2T"] for c in range(NCORES)]
        del rA, rL, rD, rN, mapsC
    out = np.concatenate([xT[c].T for c in range(NCORES)], axis=0)[None]
    return np.ascontiguousarray(out.astype(np.float32))
```

```python
import numpy as np
import ml_dtypes
from contextlib import ExitStack
import concourse.bass as bass
import concourse.mybir as mybir
from concourse.bass_utils import run_bass_kernel_spmd

F32 = mybir.dt.float32
BF16 = mybir.dt.bfloat16
AF = mybir.ActivationFunctionType
ALU = mybir.AluOpType
AX = mybir.AxisListType
NPBF = ml_dtypes.bfloat16


class Buf:
    __slots__ = ("name", "lw", "rd", "sem", "dcnt", "plw", "prd")

    def __init__(self, name):
        self.name = name
        self.lw = None
        self.rd = {}
        self.sem = None
        self.dcnt = 0
        self.plw = None
        self.prd = {}


class Prog:
    ENG = ("pe", "dve", "act", "pool", "sp")

    def __init__(self, nc, es, tag=""):
        self.nc = nc
        self.es = es
        self.tag = tag
        self.ops = {e: [] for e in self.ENG}
        self.cnt = {e: 0 for e in self.ENG}
        self.sem = {e: es.enter_context(nc.semaphore(f"s{tag}_{e}"))
                    for e in ("pe", "dve", "act", "pool")}
        self.waited = {e: {} for e in self.ENG}
        self.nsem = 0
        self.outbufs = []
        self.seq = 0

    def _need(self, eng, tok, waits):
        if tok is None:
            return
        sem, val, src = tok
        if src == "pe" and eng == "pe":
            return
        k = id(sem)
        if self.waited[eng].get(k, (None, 0))[1] >= val:
            return
        if k not in waits or waits[k][1] < val:
            waits[k] = (sem, val)

    def _commit_waits(self, eng, waits):
        for k, sv in waits.items():
            self.waited[eng][k] = sv
        return list(waits.values())

    def op(self, eng, fn, reads=(), writes=()):
        waits = {}
        for b in reads:
            self._need(eng, b.lw, waits)
        for b in writes:
            self._need(eng, b.lw, waits)
            for t in b.rd.values():
                self._need(eng, t, waits)
        self.cnt[eng] += 1
        tok = (self.sem[eng], self.cnt[eng], eng)
        wl = self._commit_waits(eng, waits)
        ws = set(id(b) for b in writes)
        for b in reads:
            if id(b) not in ws:
                b.rd[id(tok[0])] = tok
        for b in writes:
            b.lw = tok
            b.rd = {}
            b.plw = None
            b.prd = {}
        self.seq += 1
        self.ops[eng].append((wl, fn, (self.sem[eng], 1), self.seq))

    def dma(self, q, out, in_, dst, src=None, **kw):
        waits = {}
        if src is not None:
            self._need(q, src.lw, waits)
        cont = dst.lw is not None and dst.sem is not None and dst.lw[0] is dst.sem and not dst.rd
        if cont:
            self._need(q, dst.plw, waits)
            for t in dst.prd.values():
                self._need(q, t, waits)
        else:
            self._need(q, dst.lw, waits)
            for t in dst.rd.values():
                self._need(q, t, waits)
            dst.plw = dst.lw
            dst.prd = dict(dst.rd)
        if dst.sem is None:
            self.nsem += 1
            dst.sem = self.es.enter_context(self.nc.semaphore(f"d{self.tag}_{self.nsem}"))
        dst.dcnt += 16
        tok = (dst.sem, dst.dcnt, "dma")
        wl = self._commit_waits(q, waits)
        dst.lw = tok
        dst.rd = {}
        if src is not None:
            src.rd[id(dst.sem)] = tok

        def fn(e, out=out, in_=in_, kw=kw):
            return e.dma_start(out=out, in_=in_, **kw)
        self.seq += 1
        self.ops[q].append((wl, fn, (dst.sem, 16), self.seq))

    def finish(self, outbufs):
        waits = {}
        for b in outbufs:
            self._need("sp", b.lw, waits)
        for e in ("pe", "dve", "act", "pool"):
            if self.cnt[e]:
                self._need("sp", (self.sem[e], self.cnt[e], e), waits)
        wl = self._commit_waits("sp", waits)
        self.seq += 1
        self.ops["sp"].append((wl, None, None, self.seq))

    def emit(self, seg_limit=6000):
        nc = self.nc
        ops = self.ops
        allops = []
        for e in self.ENG:
            for (wl, fn, inc, seq) in ops[e]:
                allops.append((seq, e, len(wl) + (1 if fn is not None else 0)))
        allops.sort()
        cuts = []
        cnt = {e: 0 for e in self.ENG}
        for seq, e, n in allops:
            if cnt[e] + n > seg_limit:
                cuts.append(seq)
                cnt = {k: 0 for k in self.ENG}
            cnt[e] += n
        bounds = [0] + cuts + [self.seq + 1]
        pos = {e: 0 for e in self.ENG}
        for si in range(len(bounds) - 1):
            hi = bounds[si + 1]
            seg = {}
            for e in self.ENG:
                j = pos[e]
                lst = ops[e]
                k = j
                while k < len(lst) and lst[k][3] < hi:
                    k += 1
                seg[e] = lst[j:k]
                pos[e] = k

            def replay(name, e, seg=seg):
                for wl, fn, inc, seq in seg[name]:
                    for s_, v in wl:
                        e.wait_ge(s_, v)
                    if fn is not None:
                        ins = fn(e)
                        ins.then_inc(inc[0], inc[1])

            with nc.Block() as block:
                @block.sync
                def _(e):
                    replay("sp", e)

                @block.tensor
                def _(e):
                    replay("pe", e)

                @block.vector
                def _(e):
                    replay("dve", e)

                @block.scalar
                def _(e):
                    replay("act", e)

                @block.gpsimd
                def _(e):
                    replay("pool", e)


D = 1024
DFF = 2816
NF = DFF // 128
TOK = 2048
PASS = 1024
NTT = PASS // 512
EPS = 1e-6

FM32_CH = 36
FM16_CH = 24
TM_W = 1188


def chunked(W):
    K, M = W.shape
    return np.ascontiguousarray(
        W.reshape(K // 128, 128, M // 128, 128).transpose(2, 1, 0, 3).reshape(M // 128, 128, K))


def vec_pk(g):
    return np.ascontiguousarray(g.reshape(-1, 128).T)


class Ctx:
    pass


def alloc_common(nc, es, P):
    c = Ctx()
    c.nc, c.es, c.P = nc, es, P
    sb = lambda name, shape, dt: es.enter_context(nc.sbuf_tensor(name, shape, dt))
    c.sb = sb
    c.x = sb("x_sb", [128, 8, PASS], F32)
    c.bx = [[Buf(f"x{k}_{t}") for t in range(NTT)] for k in range(8)]
    c.xn = sb("xn_sb", [128, 8, PASS], BF16)
    c.bxn = [[Buf(f"xn{k}_{t}") for t in range(NTT)] for k in range(8)]
    c.sq = [sb(f"sq{i}", [128, 512], BF16) for i in range(2)]
    c.bsq = [Buf(f"sq{i}") for i in range(2)]
    c.rstd = sb("rstd", [128, 512], F32)
    c.brstd = Buf("rstd")
    c.ones = sb("ones", [128, 128], BF16)
    c.bones = Buf("ones")
    P.op("pool", lambda e: e.memset(c.ones[:], 1.0), writes=[c.bones])
    c.ps = [es.enter_context(nc.psum_tensor(f"ps{i}", [128, 512], F32)) for i in range(7)]
    c.bps = [Buf(f"ps{i}") for i in range(7)]
    c.psi = 0
    c.sqi = 0
    return c


def next_ps(c):
    i = c.psi
    c.psi = (i + 1) % len(c.ps)
    return c.ps[i], c.bps[i]


def emit_rmsnorm(c, g_sb, bg):
    P = c.P
    for t in range(NTT):
        ts = slice(t * 512, (t + 1) * 512)
        pss, bpss = next_ps(c)
        for k in range(8):
            i = c.sqi
            c.sqi = (i + 1) % 2
            sq, bsq = c.sq[i], c.bsq[i]
            P.op("act", lambda e, sq=sq, k=k, ts=ts: e.activation(out=sq[:], in_=c.x[:, k, ts], func=AF.Square),
                 reads=[c.bx[k][t]], writes=[bsq])
            P.op("pe", lambda e, sq=sq, k=k, pss=pss: e.matmul(pss[:], c.ones[:], sq[:], start=(k == 0), stop=(k == 7)),
                 reads=[bsq, c.bones], writes=[bpss])
        P.op("dve", lambda e, pss=pss: e.tensor_scalar(out=c.rstd[:], in0=pss[:], scalar1=1.0 / D, scalar2=EPS,
                                                         op0=ALU.mult, op1=ALU.add),
             reads=[bpss], writes=[c.brstd])
        P.op("act", lambda e: e.activation(out=c.rstd[:], in_=c.rstd[:], func=AF.Sqrt),
             reads=[c.brstd], writes=[c.brstd])
        P.op("dve", lambda e: e.reciprocal(out=c.rstd[:], in_=c.rstd[:]),
             reads=[c.brstd], writes=[c.brstd])
        for k in range(8):
            P.op("dve", lambda e, k=k, ts=ts: e.scalar_tensor_tensor(
                out=c.xn[:, k, ts], in0=c.x[:, k, ts], scalar=g_sb[:, k:k + 1], in1=c.rstd[:],
                op0=ALU.mult, op1=ALU.mult),
                reads=[c.bx[k][t], c.brstd, bg], writes=[c.bxn[k][t]])


def alloc_ffn(c):
    sb = c.sb
    c.g = sb("g_sb", [128, NF, PASS], BF16)
    c.bg_ = [[Buf(f"g{f}_{t}") for t in range(NTT)] for f in range(NF)]
    c.w13 = [sb(f"w13_{i}", [128, 2048], BF16) for i in range(3)]
    c.bw13 = [Buf(f"w13_{i}") for i in range(3)]
    c.w2 = [sb(f"w2_{i}", [128, DFF], BF16) for i in range(2)]
    c.bw2 = [Buf(f"w2_{i}") for i in range(2)]
    c.s = [sb(f"s_{i}", [128, 512], F32) for i in range(2)]
    c.bs = [Buf(f"s_{i}") for i in range(2)]
    c.w13i = 0
    c.w2i = 0
    c.si = 0


def emit_ffn(c, w13_d, w2_d):
    P = c.P
    for f in range(NF):
        i = c.w13i
        c.w13i = (i + 1) % 3
        w, bw = c.w13[i], c.bw13[i]
        P.dma("pool", w[:], w13_d[f], bw)
        for t in range(NTT):
            ts = slice(t * 512, (t + 1) * 512)
            p1, bp1 = next_ps(c)
            p3, bp3 = next_ps(c)
            for k in range(8):
                P.op("pe", lambda e, w=w, k=k, ts=ts, p1=p1: e.matmul(
                    p1[:], w[:, k * 128:(k + 1) * 128], c.xn[:, k, ts], start=(k == 0), stop=(k == 7)),
                    reads=[bw, c.bxn[k][t]], writes=[bp1])
            for k in range(8):
                P.op("pe", lambda e, w=w, k=k, ts=ts, p3=p3: e.matmul(
                    p3[:], w[:, 1024 + k * 128:1024 + (k + 1) * 128], c.xn[:, k, ts], start=(k == 0), stop=(k == 7)),
                    reads=[bw, c.bxn[k][t]], writes=[bp3])
            si = c.si
            c.si = (si + 1) % 2
            s, bs = c.s[si], c.bs[si]
            P.op("act", lambda e, s=s, p1=p1: e.activation(out=s[:], in_=p1[:], func=AF.Silu),
                 reads=[bp1], writes=[bs])
            P.op("dve", lambda e, s=s, p3=p3, f=f, ts=ts: e.tensor_tensor(
                out=c.g[:, f, ts], in0=s[:], in1=p3[:], op=ALU.mult),
                reads=[bs, bp3], writes=[c.bg_[f][t]])
    for d in range(8):
        i = c.w2i
        c.w2i = (i + 1) % 2
        w, bw = c.w2[i], c.bw2[i]
        P.dma("pool", w[:], w2_d[d], bw)
        for t in range(NTT):
            ts = slice(t * 512, (t + 1) * 512)
            po, bpo = next_ps(c)
            for f in range(NF):
                P.op("pe", lambda e, w=w, f=f, ts=ts, po=po: e.matmul(
                    po[:], w[:, f * 128:(f + 1) * 128], c.g[:, f, ts], start=(f == 0), stop=(f == NF - 1)),
                    reads=[bw, c.bg_[f][t]], writes=[bpo])
            P.op("dve", lambda e, d=d, ts=ts, po=po: e.scalar_tensor_tensor(
                out=c.x[:, d, ts], in0=po[:], scalar=0.5, in1=c.x[:, d, ts], op0=ALU.mult, op1=ALU.add),
                reads=[bpo], writes=[c.bx[d][t]])


def load_x(c, xT_d, t0, bsrc=None):
    for k in range(8):
        for t in range(NTT):
            c.P.dma("sp", c.x[:, k, t * 512:(t + 1) * 512],
                    xT_d[k * 128:(k + 1) * 128, t0 + t * 512:t0 + (t + 1) * 512], c.bx[k][t], src=bsrc)


def store_x(c, xT_d, t0, bdst):
    for k in range(8):
        for t in range(NTT):
            c.P.dma("sp", xT_d[k * 128:(k + 1) * 128, t0 + t * 512:t0 + (t + 1) * 512],
                    c.x[:, k, t * 512:(t + 1) * 512], bdst, src=c.bx[k][t])


def build_A():
    nc = bass.Bass("TRN2", target_bir_lowering=False)
    dt = lambda name, shape, dtype, kind: nc.dram_tensor(name, shape, dtype, kind=kind).ap()
    xT = dt("xT", [D, TOK], F32, "ExternalInput")
    w13 = dt("w13", [NF, 128, 2048], F32, "ExternalInput")
    w2 = dt("w2", [8, 128, DFF], F32, "ExternalInput")
    gf = dt("gf", [128, 8], F32, "ExternalInput")
    gm = dt("gm", [128, 8], F32, "ExternalInput")
    wfm = dt("wfm", [FM32_CH + FM16_CH, 128, 1024], F32, "ExternalInput")
    wtm = dt("wtm", [128, 8 * TM_W], F32, "ExternalInput")
    x1T = dt("x1T", [D, TOK], F32, "ExternalOutput")
    pfm32 = dt("pfm32", [FM32_CH * 128, TOK], F32, "ExternalOutput")
    pfm16 = dt("pfm16", [FM16_CH * 128, TOK], BF16, "ExternalOutput")
    ptm16 = dt("ptm16", [TOK, 1152], BF16, "ExternalOutput")
    pgate = dt("pgate", [TOK, 36], F32, "ExternalOutput")
    with ExitStack() as es:
        P = Prog(nc, es)
        c = alloc_common(nc, es, P)
        alloc_ffn(c)
        sb = c.sb
        gf_sb = sb("gf_sb", [128, 8], F32)
        gm_sb = sb("gm_sb", [128, 8], F32)
        bgf, bgm = Buf("gf"), Buf("gm")
        P.dma("sp", gf_sb[:], gf[:, :], bgf)
        P.dma("sp", gm_sb[:], gm[:, :], bgm)
        wtm_sb = sb("wtm_sb", [128, 8 * TM_W], BF16)
        bwtm = Buf("wtm")
        for k in range(8):
            P.dma("pool", wtm_sb[:, k * TM_W:(k + 1) * TM_W], wtm[:, k * TM_W:(k + 1) * TM_W], bwtm)
        wf = [sb(f"wf_{i}", [128, 1024], BF16) for i in range(3)]
        bwf = [Buf(f"wf_{i}") for i in range(3)]
        o32 = [sb(f"o32_{i}", [128, PASS], F32) for i in range(2)]
        bo32 = [Buf(f"o32_{i}") for i in range(2)]
        o16 = [sb(f"o16_{i}", [128, PASS], BF16) for i in range(2)]
        bo16 = [Buf(f"o16_{i}") for i in range(2)]
        otm = [sb(f"otm_{i}", [128, 1152], BF16) for i in range(2)]
        botm = [Buf(f"otm_{i}") for i in range(2)]
        ogt = [sb(f"ogt_{i}", [128, 36], F32) for i in range(2)]
        bogt = [Buf(f"ogt_{i}") for i in range(2)]
        bout = [Buf("o_x1"), Buf("o_fm32"), Buf("o_fm16"), Buf("o_tm"), Buf("o_gate")]
        for ps_ in range(TOK // PASS):
            t0 = ps_ * PASS
            load_x(c, xT, t0)
            emit_rmsnorm(c, gf_sb, bgf)
            emit_ffn(c, w13, w2)
            store_x(c, x1T, t0, bout[0])
            emit_rmsnorm(c, gm_sb, bgm)
            for ch in range(FM32_CH + FM16_CH):
                i = ch % 3
                w, bw = wf[i], bwf[i]
                P.dma("pool", w[:], wfm[ch], bw)
                is32 = ch < FM32_CH
                j = ch % 2
                ot, bot = (o32[j], bo32[j]) if is32 else (o16[j], bo16[j])
                if ch < 6:
                    func = AF.Copy
                elif ch < 12:
                    func = AF.Gelu_apprx_tanh
                elif ch < 36:
                    func = AF.Sigmoid
                else:
                    func = AF.Copy
                for t in range(NTT):
                    ts = slice(t * 512, (t + 1) * 512)
                    pp, bpp = next_ps(c)
                    for k in range(8):
                        P.op("pe", lambda e, w=w, k=k, ts=ts, pp=pp: e.matmul(
                            pp[:], w[:, k * 128:(k + 1) * 128], c.xn[:, k, ts], start=(k == 0), stop=(k == 7)),
                            reads=[bw, c.bxn[k][t]], writes=[bpp])
                    if func == AF.Copy and (ch + t) % 2 == 0:
                        P.op("dve", lambda e, ot=ot, pp=pp, ts=ts: e.tensor_copy(out=ot[:, ts], in_=pp[:]),
                             reads=[bpp], writes=[bot])
                    else:
                        P.op("act", lambda e, ot=ot, pp=pp, ts=ts, func=func: e.activation(
                            out=ot[:, ts], in_=pp[:], func=func), reads=[bpp], writes=[bot])
                if is32:
                    P.dma("sp", pfm32[ch * 128:(ch + 1) * 128, t0:t0 + PASS], ot[:], bout[1], src=bot)
                else:
                    c2 = ch - FM32_CH
                    P.dma("sp", pfm16[c2 * 128:(c2 + 1) * 128, t0:t0 + PASS], ot[:], bout[2], src=bot)
            for tb in range(PASS // 128):
                j = tb % 2
                t, off = divmod(tb * 128, 512)
                for (c0, cw) in ((0, 512), (512, 512), (1024, 164)):
                    pp, bpp = next_ps(c)
                    for k in range(8):
                        P.op("pe", lambda e, k=k, tb=tb, c0=c0, cw=cw, pp=pp: e.matmul(
                            pp[:, 0:cw], c.xn[:, k, tb * 128:(tb + 1) * 128],
                            wtm_sb[:, k * TM_W + c0:k * TM_W + c0 + cw], start=(k == 0), stop=(k == 7)),
                            reads=[bwtm, c.bxn[k][t]], writes=[bpp])
                    if c0 < 1024:
                        P.op("dve", lambda e, j=j, c0=c0, cw=cw, pp=pp: e.tensor_copy(
                            out=otm[j][:, c0:c0 + cw], in_=pp[:, 0:cw]), reads=[bpp], writes=[botm[j]])
                    else:
                        P.op("dve", lambda e, j=j, c0=c0, pp=pp: e.tensor_copy(
                            out=otm[j][:, c0:c0 + 128], in_=pp[:, 0:128]), reads=[bpp], writes=[botm[j]])
                        P.op("act", lambda e, j=j, pp=pp: e.activation(
                            out=ogt[j][:], in_=pp[:, 128:164], func=AF.Sigmoid), reads=[bpp], writes=[bogt[j]])
                r0 = t0 + tb * 128
                P.dma("sp", ptm16[r0:r0 + 128, :], otm[j][:], bout[3], src=botm[j])
                P.dma("sp", pgate[r0:r0 + 128, :], ogt[j][:], bout[4], src=bogt[j])
        P.finish(bout)
        P.emit()
    return nc


def w_in_layout(w_in):
    o = {}
    names = ["a_x", "a_gate", "b_q", "b_kc", "b_vc", "b_ks", "b_vs", "b_kw", "b_vw", "b_gate",
             "c_q", "c_k", "c_v", "m_a", "m_b", "m_c"]
    sizes = [768, 768, 768, 192, 192, 192, 192, 192, 192, 36, 768, 768, 768, 1024, 1024, 1024]
    off = 0
    for n, s in zip(names, sizes):
        o[n] = w_in[:, off:off + s]
        off += s
    fm = np.concatenate([o["a_x"], o["a_gate"], o["m_a"], o["m_b"], o["m_c"],
                         o["b_q"], o["b_kc"], o["b_vc"], o["b_ks"], o["b_kw"], o["c_q"], o["c_k"]], axis=1)
    tm = np.concatenate([o["b_vs"], o["b_vw"], o["c_v"], o["b_gate"]], axis=1)
    assert fm.shape[1] == 60 * 128 and tm.shape[1] == TM_W
    wfm = chunked(fm)
    wtm = np.ascontiguousarray(tm.reshape(8, 128, TM_W).transpose(1, 0, 2).reshape(128, 8 * TM_W))
    return wfm, wtm


def ffn_layout(w1, w3, w2):
    w13 = np.concatenate([chunked(w1), chunked(w3)], axis=2)
    return np.ascontiguousarray(w13), chunked(w2)


def emit_final_norm(c, g_sb, bg):
    P = c.P
    for t in range(NTT):
        ts = slice(t * 512, (t + 1) * 512)
        pss, bpss = next_ps(c)
        for k in range(8):
            i = c.sqi
            c.sqi = (i + 1) % 2
            sq, bsq = c.sq[i], c.bsq[i]
            P.op("act", lambda e, sq=sq, k=k, ts=ts: e.activation(out=sq[:], in_=c.x[:, k, ts], func=AF.Square),
                 reads=[c.bx[k][t]], writes=[bsq])
            P.op("pe", lambda e, sq=sq, k=k, pss=pss: e.matmul(pss[:], c.ones[:], sq[:], start=(k == 0), stop=(k == 7)),
                 reads=[bsq, c.bones], writes=[bpss])
        P.op("dve", lambda e, pss=pss: e.tensor_scalar(out=c.rstd[:], in0=pss[:], scalar1=1.0 / D, scalar2=EPS,
                                                         op0=ALU.mult, op1=ALU.add),
             reads=[bpss], writes=[c.brstd])
        P.op("act", lambda e: e.activation(out=c.rstd[:], in_=c.rstd[:], func=AF.Sqrt),
             reads=[c.brstd], writes=[c.brstd])
        P.op("dve", lambda e: e.reciprocal(out=c.rstd[:], in_=c.rstd[:]),
             reads=[c.brstd], writes=[c.brstd])
        for k in range(8):
            P.op("dve", lambda e, k=k, ts=ts: e.scalar_tensor_tensor(
                out=c.x[:, k, ts], in0=c.x[:, k, ts], scalar=g_sb[:, k:k + 1], in1=c.rstd[:],
                op0=ALU.mult, op1=ALU.mult),
                reads=[c.brstd, bg], writes=[c.bx[k][t]])


def build_C(last):
    nc = bass.Bass("TRN2", target_bir_lowering=False)
    dt = lambda name, shape, dtype, kind: nc.dram_tensor(name, shape, dtype, kind=kind).ap()
    xT = dt("xT", [D, TOK], F32, "ExternalInput")
    pfm32 = dt("pfm32", [FM32_CH * 128, TOK], F32, "ExternalInput")
    hloc = dt("hloc", [768, TOK], F32, "ExternalInput")
    pcum = dt("pcum", [768, TOK], F32, "ExternalInput")
    ends = dt("ends", [128, 6 * 2 * 8], F32, "ExternalInput")
    csel = dt("csel", [128, 8], F32, "ExternalInput")
    obT = dt("obT", [768, TOK], BF16, "ExternalInput")
    odT = dt("odT", [256, TOK], BF16, "ExternalInput")
    wup = dt("wup", [8, 128, 14 * 128], F32, "ExternalInput")
    wout = dt("wout", [8, 128, 1024], F32, "ExternalInput")
    w13 = dt("w13", [NF, 128, 2048], F32, "ExternalInput")
    w2 = dt("w2", [8, 128, DFF], F32, "ExternalInput")
    gf = dt("gf", [128, 8], F32, "ExternalInput")
    gl = dt("gl", [128, 8], F32, "ExternalInput")
    x2T = dt("x2T", [D, TOK], F32, "ExternalOutput")
    with ExitStack() as es:
        P = Prog(nc, es)
        c = alloc_common(nc, es, P)
        alloc_ffn(c)
        sb = c.sb
        gf_sb = sb("gf_sb", [128, 8], F32)
        gl_sb = sb("gl_sb", [128, 8], F32)
        bgf, bgl = Buf("gf"), Buf("gl")
        P.dma("sp", gf_sb[:], gf[:, :], bgf)
        P.dma("sp", gl_sb[:], gl[:, :], bgl)
        ends_sb = sb("ends_sb", [128, 96], F32)
        csel_sb = sb("csel_sb", [128, 8], F32)
        bends, bcsel = Buf("ends"), Buf("csel")
        P.dma("sp", ends_sb[:], ends[:, :], bends)
        P.dma("sp", csel_sb[:], csel[:, :], bcsel)
        car = sb("car", [128, 8], F32)
        bcar = Buf("car")
        cin = sb("cin", [128, 6], F32)
        bcin = Buf("cin")
        for ch in range(6):
            o = ch * 16
            P.op("dve", lambda e, o=o: e.tensor_tensor_scan(
                out=car[:], data0=ends_sb[:, o:o + 8], data1=ends_sb[:, o + 8:o + 16], initial=0.0,
                op0=ALU.mult, op1=ALU.add), reads=[bends], writes=[bcar])
            P.op("dve", lambda e: e.tensor_tensor(out=car[:], in0=car[:], in1=csel_sb[:], op=ALU.mult),
                 reads=[bcar, bcsel], writes=[bcar])
            P.op("dve", lambda e, ch=ch: e.reduce_sum(out=cin[:, ch:ch + 1], in_=car[:], axis=AX.X),
                 reads=[bcar], writes=[bcin])
        lt = [[sb(f"lt{j}_{i}", [128, 512], F32) for i in range(2)] for j in range(3)]
        blt = [[Buf(f"lt{j}_{i}") for i in range(2)] for j in range(3)]
        sm = [[sb(f"sm{j}_{i}", [128, 512], F32) for i in range(2)] for j in range(3)]
        bsm = [[Buf(f"sm{j}_{i}") for i in range(2)] for j in range(3)]
        mm = [sb(f"mm{j}", [128, 512], F32) for j in range(3)]
        bmm = [Buf(f"mm{j}") for j in range(3)]
        wu = [sb(f"wu{i}", [128, 14 * 128], BF16) for i in range(2)]
        bwu = [Buf(f"wu{i}") for i in range(2)]
        bout = Buf("o_x2")
        li = 0
        for ps_ in range(TOK // PASS):
            t0 = ps_ * PASS
            load_x(c, xT, t0)
            for ch in range(6):
                for t in range(NTT):
                    ts = slice(t * 512, (t + 1) * 512)
                    gs = slice(t0 + t * 512, t0 + (t + 1) * 512)
                    i = li % 2
                    li += 1
                    P.dma("sp", lt[0][i][:], hloc[ch * 128:(ch + 1) * 128, gs], blt[0][i])
                    P.dma("sp", lt[1][i][:], pcum[ch * 128:(ch + 1) * 128, gs], blt[1][i])
                    P.dma("sp", lt[2][i][:], pfm32[768 + ch * 128:768 + (ch + 1) * 128, gs], blt[2][i])
                    P.op("dve", lambda e, i=i, ch=ch: e.scalar_tensor_tensor(
                        out=lt[0][i][:], in0=lt[1][i][:], scalar=cin[:, ch:ch + 1], in1=lt[0][i][:],
                        op0=ALU.mult, op1=ALU.add), reads=[blt[1][i], bcin], writes=[blt[0][i]])
                    P.op("dve", lambda e, i=i, ch=ch, ts=ts: e.tensor_tensor(
                        out=c.g[:, 8 + ch, ts], in0=lt[0][i][:], in1=lt[2][i][:], op=ALU.mult),
                        reads=[blt[0][i], blt[2][i]], writes=[c.bg_[8 + ch][t]])
                    P.dma("sp", c.g[:, 14 + ch, ts], obT[ch * 128:(ch + 1) * 128, gs], c.bg_[14 + ch][t])
            for ch in range(2):
                for t in range(NTT):
                    ts = slice(t * 512, (t + 1) * 512)
                    gs = slice(t0 + t * 512, t0 + (t + 1) * 512)
                    P.dma("sp", c.g[:, 20 + ch, ts], odT[ch * 128:(ch + 1) * 128, gs], c.bg_[20 + ch][t])
            for d in range(8):
                w, bw = wu[d % 2], bwu[d % 2]
                P.dma("pool", w[:], wup[d], bw)
                for t in range(NTT):
                    ts = slice(t * 512, (t + 1) * 512)
                    gs = slice(t0 + t * 512, t0 + (t + 1) * 512)
                    i = li % 2
                    li += 1
                    for j in range(3):
                        r0 = 1536 + j * 1024 + d * 128
                        P.dma("sp", sm[j][i][:], pfm32[r0:r0 + 128, gs], bsm[j][i])
                    pa, bpa = next_ps(c)
                    pb, bpb = next_ps(c)
                    pc, bpc = next_ps(c)
                    for k in range(6):
                        P.op("pe", lambda e, w=w, k=k, ts=ts, pa=pa: e.matmul(
                            pa[:], w[:, k * 128:(k + 1) * 128], c.g[:, 8 + k, ts], start=(k == 0), stop=(k == 5)),
                            reads=[bw, c.bg_[8 + k][t]], writes=[bpa])
                    for k in range(6):
                        P.op("pe", lambda e, w=w, k=k, ts=ts, pb=pb: e.matmul(
                            pb[:], w[:, (6 + k) * 128:(7 + k) * 128], c.g[:, 14 + k, ts], start=(k == 0), stop=(k == 5)),
                            reads=[bw, c.bg_[14 + k][t]], writes=[bpb])
                    for k in range(2):
                        P.op("pe", lambda e, w=w, k=k, ts=ts, pc=pc: e.matmul(
                            pc[:], w[:, (12 + k) * 128:(13 + k) * 128], c.g[:, 20 + k, ts], start=(k == 0), stop=(k == 1)),
                            reads=[bw, c.bg_[20 + k][t]], writes=[bpc])
                    for j, (pp, bpp) in enumerate(((pa, bpa), (pb, bpb), (pc, bpc))):
                        P.op("dve", lambda e, j=j, i=i, pp=pp: e.tensor_tensor(
                            out=mm[j][:], in0=sm[j][i][:], in1=pp[:], op=ALU.mult),
                            reads=[bsm[j][i], bpp], writes=[bmm[j]])
                    P.op("pool", lambda e: e.tensor_tensor(out=mm[0][:], in0=mm[0][:], in1=mm[1][:], op=ALU.add),
                         reads=[bmm[1]], writes=[bmm[0]])
                    P.op("pool", lambda e, d=d, ts=ts: e.tensor_tensor(
                        out=c.g[:, d, ts], in0=mm[0][:], in1=mm[2][:], op=ALU.add),
                        reads=[bmm[0], bmm[2]], writes=[c.bg_[d][t]])
            for d in range(8):
                i = c.w13i
                c.w13i = (i + 1) % 3
                w, bw = c.w13[i], c.bw13[i]
                P.dma("pool", w[:, 0:1024], wout[d], bw)
                for t in range(NTT):
                    ts = slice(t * 512, (t + 1) * 512)
                    po, bpo = next_ps(c)
                    for k in range(8):
                        P.op("pe", lambda e, w=w, k=k, ts=ts, po=po: e.matmul(
                            po[:], w[:, k * 128:(k + 1) * 128], c.g[:, k, ts], start=(k == 0), stop=(k == 7)),
                            reads=[bw, c.bg_[k][t]], writes=[bpo])
                    P.op("dve", lambda e, d=d, ts=ts, po=po: e.tensor_tensor(
                        out=c.x[:, d, ts], in0=po[:], in1=c.x[:, d, ts], op=ALU.add),
                        reads=[bpo], writes=[c.bx[d][t]])
            emit_rmsnorm(c, gf_sb, bgf)
            emit_ffn(c, w13, w2)
            if last:
                emit_final_norm(c, gl_sb, bgl)
            store_x(c, x2T, t0, bout)
        P.finish([bout])
        P.emit()
    return nc


TOK = 2048
LRU_C = 8.0


def build_L():
    nc = bass.Bass("TRN2", target_bir_lowering=False)
    dt = lambda name, shape, dtype, kind: nc.dram_tensor(name, shape, dtype, kind=kind).ap()
    axh = dt("axh", [768, 3 + TOK], F32, "ExternalInput")
    cw = dt("cw", [128, 24], F32, "ExternalInput")
    cb = dt("cb", [128, 6], F32, "ExternalInput")
    wa = dt("wa", [128, 6 * 128], F32, "ExternalInput")
    wi = dt("wi", [128, 6 * 128], F32, "ExternalInput")
    ba = dt("ba", [128, 6], F32, "ExternalInput")
    bi = dt("bi", [128, 6], F32, "ExternalInput")
    lam = dt("lam", [128, 6], F32, "ExternalInput")
    hloc = dt("hloc", [768, TOK], F32, "ExternalOutput")
    pcum = dt("pcum", [768, TOK], F32, "ExternalOutput")
    with ExitStack() as es:
        P = Prog(nc, es)
        sb = lambda name, shape, d: es.enter_context(nc.sbuf_tensor(name, shape, d))
        small = {}
        for name, ap, w in (("cw", cw, 24), ("cb", cb, 6), ("ba", ba, 6), ("bi", bi, 6), ("lam", lam, 6)):
            t = sb(name + "_sb", [128, w], F32)
            b = Buf(name)
            P.dma("sp", t[:], ap[:, :], b)
            small[name] = (t, b)
        wa_sb = sb("wa_sb", [128, 768], BF16)
        wi_sb = sb("wi_sb", [128, 768], BF16)
        bwa, bwi = Buf("wa"), Buf("wi")
        P.dma("pool", wa_sb[:], wa[:, :], bwa)
        P.dma("pool", wi_sb[:], wi[:, :], bwi)
        lam_sb, blam = small["lam"]
        nsp = sb("nsp", [128, 6], F32)
        nsp2 = sb("nsp2", [128, 6], F32)
        bnsp, bnsp2 = Buf("nsp"), Buf("nsp2")
        P.op("act", lambda e: e.activation(out=nsp[:], in_=lam_sb[:], func=AF.Exp, scale=-1.0),
             reads=[blam], writes=[bnsp])
        P.op("dve", lambda e: e.tensor_scalar_add(out=nsp[:], in0=nsp[:], scalar1=1.0), reads=[bnsp], writes=[bnsp])
        P.op("act", lambda e: e.activation(out=nsp[:], in_=nsp[:], func=AF.Ln), reads=[bnsp], writes=[bnsp])
        P.op("dve", lambda e: e.tensor_scalar_mul(out=nsp2[:], in0=nsp[:], scalar1=-2.0 * LRU_C),
             reads=[bnsp], writes=[bnsp2])
        P.op("dve", lambda e: e.tensor_scalar_mul(out=nsp[:], in0=nsp[:], scalar1=-LRU_C),
             reads=[bnsp], writes=[bnsp])
        zeros = sb("zeros", [128, TOK], F32)
        bz = Buf("zeros")
        P.op("pool", lambda e: e.memset(zeros[:], 0.0), writes=[bz])
        ax = [sb(f"ax{i}", [128, 3 + TOK], F32) for i in range(2)]
        bax = [Buf(f"ax{i}") for i in range(2)]
        xc = [sb(f"xc{i}", [128, TOK], F32) for i in range(2)]
        bxc = [Buf(f"xc{i}") for i in range(2)]
        xcb = [sb(f"xcb{i}", [128, TOK], BF16) for i in range(2)]
        bxcb = [Buf(f"xcb{i}") for i in range(2)]
        rr = [sb(f"rr{i}", [128, TOK], F32) for i in range(2)]
        brr = [Buf(f"rr{i}") for i in range(2)]
        ii = [sb(f"ii{i}", [128, TOK], F32) for i in range(2)]
        bii = [Buf(f"ii{i}") for i in range(2)]
        aa = [sb(f"aa{i}", [128, TOK], F32) for i in range(2)]
        baa = [Buf(f"aa{i}") for i in range(2)]
        uu = [sb(f"uu{i}", [128, TOK], F32) for i in range(2)]
        buu = [Buf(f"uu{i}") for i in range(2)]
        hh = [sb(f"hh{i}", [128, TOK], F32) for i in range(2)]
        bhh = [Buf(f"hh{i}") for i in range(2)]
        pp_ = [sb(f"pp{i}", [128, TOK], F32) for i in range(2)]
        bpp_ = [Buf(f"pp{i}") for i in range(2)]
        ps = [es.enter_context(nc.psum_tensor(f"ps{i}", [128, 512], F32)) for i in range(4)]
        bps = [Buf(f"ps{i}") for i in range(4)]
        psi = 0
        bo1, bo2 = Buf("o_h"), Buf("o_p")
        cw_sb, bcw = small["cw"]
        cb_sb, bcb = small["cb"]
        ba_sb, bba = small["ba"]
        bi_sb, bbi = small["bi"]
        for ch in range(6):
            i = ch % 2
            P.dma("sp", ax[i][:], axh[ch * 128:(ch + 1) * 128, :], bax[i])
            P.op("dve", lambda e, i=i, ch=ch: e.tensor_scalar(
                out=xc[i][:], in0=ax[i][:, 0:TOK], scalar1=cw_sb[:, ch * 4:ch * 4 + 1], scalar2=cb_sb[:, ch:ch + 1],
                op0=ALU.mult, op1=ALU.add), reads=[bax[i], bcw, bcb], writes=[bxc[i]])
            for j in range(1, 4):
                P.op("dve", lambda e, i=i, ch=ch, j=j: e.scalar_tensor_tensor(
                    out=xc[i][:], in0=ax[i][:, j:j + TOK], scalar=cw_sb[:, ch * 4 + j:ch * 4 + j + 1], in1=xc[i][:],
                    op0=ALU.mult, op1=ALU.add), reads=[bax[i], bcw], writes=[bxc[i]])
            P.op("pool", lambda e, i=i: e.tensor_copy(out=xcb[i][:], in_=xc[i][:]), reads=[bxc[i]], writes=[bxcb[i]])
            for t in range(TOK // 512):
                ts = slice(t * 512, (t + 1) * 512)
                for (w_sb, bw, b_sb, bb, dst, bdst) in ((wa_sb, bwa, ba_sb, bba, rr[i], brr[i]),
                                                         (wi_sb, bwi, bi_sb, bbi, ii[i], bii[i])):
                    p, bp = ps[psi], bps[psi]
                    psi = (psi + 1) % 4
                    P.op("pe", lambda e, w_sb=w_sb, ch=ch, i=i, ts=ts, p=p: e.matmul(
                        p[:], w_sb[:, ch * 128:(ch + 1) * 128], xcb[i][:, ts], start=True, stop=True),
                        reads=[bw, bxcb[i]], writes=[bp])
                    P.op("act", lambda e, dst=dst, ts=ts, p=p, b_sb=b_sb, ch=ch: e.activation(
                        out=dst[:, ts], in_=p[:], func=AF.Sigmoid, bias=b_sb[:, ch:ch + 1]),
                        reads=[bp, bb], writes=[bdst])
            P.op("act", lambda e, i=i, ch=ch: e.activation(out=aa[i][:], in_=rr[i][:], func=AF.Exp,
                                                            scale=nsp[:, ch:ch + 1]),
                 reads=[brr[i], bnsp], writes=[baa[i]])
            P.op("act", lambda e, i=i, ch=ch: e.activation(out=uu[i][:], in_=rr[i][:], func=AF.Exp,
                                                            scale=nsp2[:, ch:ch + 1]),
                 reads=[brr[i], bnsp2], writes=[buu[i]])
            P.op("dve", lambda e, i=i: e.tensor_scalar(out=uu[i][:], in0=uu[i][:], scalar1=-1.0, scalar2=1.0,
                                                        op0=ALU.mult, op1=ALU.add), reads=[buu[i]], writes=[buu[i]])
            P.op("act", lambda e, i=i: e.activation(out=uu[i][:], in_=uu[i][:], func=AF.Sqrt),
                 reads=[buu[i]], writes=[buu[i]])
            P.op("pool", lambda e, i=i: e.tensor_tensor(out=ii[i][:], in0=ii[i][:], in1=xc[i][:], op=ALU.mult),
                 reads=[bxc[i]], writes=[bii[i]])
            P.op("dve", lambda e, i=i: e.tensor_tensor(out=uu[i][:], in0=uu[i][:], in1=ii[i][:], op=ALU.mult),
                 reads=[bii[i]], writes=[buu[i]])
            P.op("dve", lambda e, i=i: e.tensor_tensor_scan(
                out=hh[i][:], data0=aa[i][:], data1=uu[i][:], initial=0.0, op0=ALU.mult, op1=ALU.add),
                reads=[baa[i], buu[i]], writes=[bhh[i]])
            P.op("dve", lambda e, i=i: e.tensor_tensor_scan(
                out=pp_[i][:], data0=aa[i][:], data1=zeros[:], initial=1.0, op0=ALU.mult, op1=ALU.add),
                reads=[baa[i], bz], writes=[bpp_[i]])
            P.dma("sp", hloc[ch * 128:(ch + 1) * 128, :], hh[i][:], bo1, src=bhh[i])
            P.dma("sp", pcum[ch * 128:(ch + 1) * 128, :], pp_[i][:], bo2, src=bpp_[i])
        kcp = dt("kcp", [64, 3 * 128], BF16, "ExternalOutput")
        vcp = dt("vcp", [128, 3 * 64], BF16, "ExternalOutput")
        bo3, bo4 = Buf("o_kc"), Buf("o_vc")
        kcp_sb = sb("kcp_sb", [64, 384], BF16)
        vcp_sb = sb("vcp_sb", [128, 192], BF16)
        bkcp, bvcp = Buf("kcp"), Buf("vcp")
        hid = [sb(f"hid{i}", [128, 256], BF16) for i in range(2)]
        bhid = [Buf(f"hid{i}") for i in range(2)]
        hi = 0
        for kv in range(2):
            k2 = dt(f"k2_{kv}", [128, 3 * 2080], BF16, "ExternalInput")
            w1 = dt(f"cw1_{kv}", [128, 16 * 256], F32, "ExternalInput")
            w2 = dt(f"cw2_{kv}", [128, 2 * 64], F32, "ExternalInput")
            pos2 = dt(f"pos2_{kv}", [128, 16], F32, "ExternalInput")
            k2_sb = sb(f"k2_sb{kv}", [128, 3 * 2080], BF16)
            w1_sb = sb(f"cw1_sb{kv}", [128, 16 * 256], BF16)
            w2_sb = sb(f"cw2_sb{kv}", [128, 128], BF16)
            pos_sb = sb(f"pos_sb{kv}", [128, 16], BF16)
            bias_sb = sb(f"cbias{kv}", [128, 2], F32)
            bk2, bw1, bw2, bpos, bbias = Buf("k2"), Buf("cw1"), Buf("cw2"), Buf("pos"), Buf("cbias")
            P.dma("sp", k2_sb[:], k2[:, :], bk2)
            P.dma("pool", w1_sb[:], w1[:, :], bw1)
            P.dma("pool", w2_sb[:], w2[:, :], bw2)
            P.dma("pool", pos_sb[:], pos2[:, :], bpos)
            for hc in range(2):
                p, bp = ps[psi], bps[psi]
                psi = (psi + 1) % 4
                for lp in range(16):
                    P.op("pe", lambda e, p=p, lp=lp, hc=hc, w1_sb=w1_sb, pos_sb=pos_sb: e.matmul(
                        p[:, 0:1], w1_sb[:, lp * 256 + hc * 128:lp * 256 + (hc + 1) * 128], pos_sb[:, lp:lp + 1],
                        start=(lp == 0), stop=(lp == 15)), reads=[bw1, bpos], writes=[bp])
                P.op("dve", lambda e, p=p, hc=hc, bias_sb=bias_sb: e.tensor_copy(out=bias_sb[:, hc:hc + 1], in_=p[:, 0:1]),
                     reads=[bp], writes=[bbias])
            for kvh in range(3):
                hd, bhd = hid[hi % 2], bhid[hi % 2]
                hi += 1
                for hc in range(2):
                    p, bp = ps[psi], bps[psi]
                    psi = (psi + 1) % 4
                    for lp in range(16):
                        c0 = kvh * 2080 + 2 * lp
                        P.op("pe", lambda e, p=p, lp=lp, hc=hc, c0=c0, w1_sb=w1_sb, k2_sb=k2_sb: e.matmul(
                            p[:, 0:128], w1_sb[:, lp * 256 + hc * 128:lp * 256 + (hc + 1) * 128],
                            k2_sb[:, c0:c0 + 2048:16], start=(lp == 0), stop=(lp == 15)),
                            reads=[bw1, bk2], writes=[bp])
                    P.op("act", lambda e, p=p, hc=hc, hd=hd, bias_sb=bias_sb: e.activation(
                        out=hd[:, hc * 128:(hc + 1) * 128], in_=p[:, 0:128], func=AF.Gelu_apprx_tanh,
                        bias=bias_sb[:, hc:hc + 1]), reads=[bp, bbias], writes=[bhd])
                p, bp = ps[psi], bps[psi]
                psi = (psi + 1) % 4
                if kv == 0:
                    for hc in range(2):
                        P.op("pe", lambda e, p=p, hc=hc, hd=hd, w2_sb=w2_sb: e.matmul(
                            p[0:64, 0:128], w2_sb[:, hc * 64:(hc + 1) * 64], hd[:, hc * 128:(hc + 1) * 128],
                            start=(hc == 0), stop=(hc == 1)), reads=[bw2, bhd], writes=[bp])
                    P.op("dve", lambda e, p=p, kvh=kvh: e.tensor_copy(out=kcp_sb[:, kvh * 128:(kvh + 1) * 128], in_=p[0:64, 0:128]),
                         reads=[bp], writes=[bkcp])
                else:
                    for hc in range(2):
                        P.op("pe", lambda e, p=p, hc=hc, hd=hd, w2_sb=w2_sb: e.matmul(
                            p[:, 0:64], hd[:, hc * 128:(hc + 1) * 128], w2_sb[:, hc * 64:(hc + 1) * 64],
                            start=(hc == 0), stop=(hc == 1)), reads=[bw2, bhd], writes=[bp])
                    P.op("dve", lambda e, p=p, kvh=kvh: e.tensor_copy(out=vcp_sb[:, kvh * 64:(kvh + 1) * 64], in_=p[:, 0:64]),
                         reads=[bp], writes=[bvcp])
        P.dma("sp", kcp[:, :], kcp_sb[:], bo3, src=bkcp)
        P.dma("sp", vcp[:, :], vcp_sb[:], bo4, src=bvcp)
        P.finish([bo1, bo2, bo3, bo4])
        P.emit()
    return nc


def cmp_host_inputs(inp, l, kcT, vcT, core):
    d = {}
    t0 = core * TOK
    for kv, (src, w1n, w2n, pn) in enumerate(((kcT, "cmp_k_w1", "cmp_k_w2", "cmp_pos_k"),
                                              (vcT, "cmp_v_w1", "cmp_v_w2", "cmp_pos_v"))):
        S = src.shape[1]
        seg = np.zeros((192, 2081), NPBF)
        n = min(2081, S - t0)
        seg[:, :n] = src[:, t0:t0 + n]
        a = seg[:, 0:2080].reshape(3, 64, 2080)
        b = seg[:, 1:2081].reshape(3, 64, 2080)
        k2 = np.concatenate([a, b], axis=1)
        d[f"k2_{kv}"] = np.ascontiguousarray(k2.transpose(1, 0, 2).reshape(128, 3 * 2080))
        w1 = inp[w1n][l]
        d[f"cw1_{kv}"] = np.ascontiguousarray(w1.reshape(16, 128, 256).transpose(1, 0, 2).reshape(128, 16 * 256))
        w2 = inp[w2n][l]
        d[f"cw2_{kv}"] = np.ascontiguousarray(w2.reshape(2, 128, 64).transpose(1, 0, 2).reshape(128, 128))
        pos = inp[pn][l].reshape(2048)
        d[f"pos2_{kv}"] = np.ascontiguousarray(pos.reshape(16, 128).T)
    return d


def lru_host_inputs(inp, l):
    pk = lambda v: np.ascontiguousarray(v.reshape(-1, 128).T)
    cwv = inp["conv_w"][l]
    cw = np.ascontiguousarray(cwv.reshape(4, 6, 128).transpose(2, 1, 0).reshape(128, 24))
    wa = np.ascontiguousarray(inp["lru_wa"][l].transpose(1, 0, 2).reshape(128, 768))
    wi = np.ascontiguousarray(inp["lru_wi"][l].transpose(1, 0, 2).reshape(128, 768))
    return {"cw": cw, "cb": pk(inp["conv_b"][l]), "wa": wa, "wi": wi, "ba": pk(inp["lru_ba"][l]),
            "bi": pk(inp["lru_bi"][l]), "lam": pk(inp["lru_lambda"][l])}


TOK = 2048
NT = TOK // 128
DIL = ((128, 1), (512, 4), (2048, 16))
NB = (2, 5, 17)
HALO = (128, 512, 2048)
KLEN = tuple(h + TOK for h in HALO)
NBLK = tuple(k // 128 for k in KLEN)
BIG = 30000.0


def dil_masks():
    kk = np.arange(128)[:, None]
    qq = np.arange(128)[None, :]
    ms = []
    for g, (W, dil) in enumerate(DIL):
        for rr in range(NB[g]):
            dist = (NB[g] - 1 - rr) * 128 + qq - kk
            ok = (dist >= 0) & (dist <= W) & (dist % dil == 0)
            m = np.where(ok, 0.0, -BIG).astype(np.float32)
            ms.append(np.tile(m, (1, 4)))
    return np.ascontiguousarray(np.stack(ms, 1)).astype(NPBF)


def build_D():
    nc = bass.Bass("TRN2", target_bir_lowering=False)
    dt = lambda name, shape, dtype, kind: nc.dram_tensor(name, shape, dtype, kind=kind).ap()
    cq = dt("cq", [64, NT, 12 * 128], BF16, "ExternalInput")
    ck = [dt(f"ck{g}", [64, 4 * KLEN[g]], BF16, "ExternalInput") for g in range(3)]
    cv = [dt(f"cv{g}", [128, NBLK[g] * 260], BF16, "ExternalInput") for g in range(3)]
    db = dt("db", [128, 24 * 512], BF16, "ExternalInput")
    ident = dt("ident", [128, 128], BF16, "ExternalInput")
    odT = dt("odT", [256, TOK], BF16, "ExternalOutput")
    with ExitStack() as es:
        P = Prog(nc, es)
        sb = lambda name, shape, d: es.enter_context(nc.sbuf_tensor(name, shape, d))
        ck_sb = [sb(f"ck_sb{g}", [64, 4 * KLEN[g]], BF16) for g in range(3)]
        bck = [Buf(f"ck{g}") for g in range(3)]
        cv_sb = [sb(f"cv_sb{g}", [128, NBLK[g] * 260], BF16) for g in range(3)]
        bcv = [Buf(f"cv{g}") for g in range(3)]
        for g in range(3):
            for hj in range(4):
                P.dma("sp", ck_sb[g][:, hj * KLEN[g]:(hj + 1) * KLEN[g]], ck[g][:, hj * KLEN[g]:(hj + 1) * KLEN[g]], bck[g])
            P.dma("sp", cv_sb[g][:], cv[g][:, :], bcv[g])
        db_sb = sb("db_sb", [128, 24 * 512], BF16)
        bdb = Buf("db")
        for j in range(4):
            P.dma("sp", db_sb[:, j * 3072:(j + 1) * 3072], db[:, j * 3072:(j + 1) * 3072], bdb)
        id_sb = sb("id_sb", [128, 128], BF16)
        bid = Buf("id")
        P.dma("sp", id_sb[:], ident[:, :], bid)
        cq_sb = [sb(f"cq_sb{i}", [64, 12 * 128], BF16) for i in range(2)]
        bcq = [Buf(f"cq{i}") for i in range(2)]
        pT = [sb(f"pT{i}", [128, 512], BF16) for i in range(3)]
        bpT = [Buf(f"pT{i}") for i in range(3)]
        ps = [es.enter_context(nc.psum_tensor(f"ps{i}", [128, 512], F32)) for i in range(3)]
        bps = [Buf(f"ps{i}") for i in range(3)]
        po = [es.enter_context(nc.psum_tensor(f"po{i}", [128, 260], F32)) for i in range(2)]
        bpo = [Buf(f"po{i}") for i in range(2)]
        ptr = es.enter_context(nc.psum_tensor("ptr", [128, 256], BF16))
        bptr = Buf("ptr")
        rden = sb("rden", [128, 4], F32)
        brden = Buf("rden")
        ob = sb("ob", [128, 256], BF16)
        bob = Buf("ob")
        oT = [sb(f"oT{i}", [128, 256], BF16) for i in range(2)]
        boT = [Buf(f"oT{i}") for i in range(2)]
        bout = Buf("o_od")
        it = 0
        for i in range(NT):
            q_sb, bq = cq_sb[i % 2], bcq[i % 2]
            P.dma("sp", q_sb[:], cq[:, i, :], bq)
            o_ps, bo_ps = po[i % 2], bpo[i % 2]
            mi = 0
            total = sum(NB)
            n = 0
            def emit_scores(g, rr, mi):
                nonlocal it
                kb = i + rr
                p, bp = ps[it % 3], bps[it % 3]
                t_sb, bt = pT[it % 3], bpT[it % 3]
                it += 1
                for hj in range(4):
                    P.op("pe", lambda e, p=p, g=g, hj=hj, kb=kb, q_sb=q_sb: e.matmul(
                        p[:, hj * 128:(hj + 1) * 128],
                        ck_sb[g][:, hj * KLEN[g] + kb * 128:hj * KLEN[g] + (kb + 1) * 128],
                        q_sb[:, (g * 4 + hj) * 128:(g * 4 + hj + 1) * 128], start=(hj == 0), stop=False, skip_group_check=True),
                        reads=[bck[g], bq], writes=[bp])
                P.op("pe", lambda e, p=p, mi=mi: e.matmul(
                    p[:], id_sb[:], db_sb[:, mi * 512:(mi + 1) * 512], start=False, stop=True, skip_group_check=True),
                    reads=[bid, bdb], writes=[bp])
                P.op("act", lambda e, p=p, t_sb=t_sb: e.activation(out=t_sb[:], in_=p[:], func=AF.Exp, scale=0.125),
                     reads=[bp], writes=[bt])
                return (t_sb, bt, g, kb)

            def emit_pv(ctx, n):
                t_sb, bt, g, kb = ctx
                for hj in range(4):
                    P.op("pe", lambda e, o_ps=o_ps, t_sb=t_sb, hj=hj, g=g, kb=kb, n=n: e.matmul(
                        o_ps[:, hj * 65:(hj + 1) * 65], t_sb[:, hj * 128:(hj + 1) * 128],
                        cv_sb[g][:, kb * 260 + hj * 65:kb * 260 + (hj + 1) * 65],
                        start=(n == 0 and hj == 0), stop=(n == total - 1), skip_group_check=True),
                        reads=[bt, bcv[g]], writes=[bo_ps])

            prev = None
            for g in range(3):
                for rr in range(NB[g]):
                    cur = emit_scores(g, rr, mi)
                    if prev is not None:
                        emit_pv(prev, n)
                        n += 1
                    prev = cur
                    mi += 1
            emit_pv(prev, n)
            o3 = o_ps[:].rearrange("p (h e) -> p h e", e=65)
            P.op("dve", lambda e, o3=o3: e.reciprocal(out=rden[:], in_=o3[:, :, 64]), reads=[bo_ps], writes=[brden])
            for hj in range(4):
                P.op("dve", lambda e, o_ps=o_ps, hj=hj: e.tensor_scalar_mul(
                    out=ob[:, hj * 64:(hj + 1) * 64], in0=o_ps[:, hj * 65:hj * 65 + 64], scalar1=rden[:, hj:hj + 1]),
                    reads=[bo_ps, brden], writes=[bob])
            for cc in range(2):
                P.op("pe", lambda e, cc=cc: e.transpose(ptr[:, cc * 128:(cc + 1) * 128], ob[:, cc * 128:(cc + 1) * 128], id_sb[:]),
                     reads=[bob, bid], writes=[bptr])
            o_t, bo_t = oT[i % 2], boT[i % 2]
            P.op("act", lambda e, o_t=o_t: e.activation(out=o_t[:], in_=ptr[:], func=AF.Copy), reads=[bptr], writes=[bo_t])
            for cc in range(2):
                P.dma("sp", odT[cc * 128:(cc + 1) * 128, i * 128:(i + 1) * 128], o_t[:, cc * 128:(cc + 1) * 128],
                      bout, src=bo_t)
        P.finish([bout])
        P.emit()
    return nc


def dil_host_inputs(cqT, ckT, cv, core):
    S = cqT.shape[1]
    t0 = core * TOK
    d = {}
    q = cqT[:, t0:t0 + TOK].reshape(12, 64, NT, 128)
    d["cq"] = np.ascontiguousarray(q.transpose(1, 2, 0, 3).reshape(64, NT, 12 * 128))
    for g in range(3):
        h = HALO[g]
        lo = t0 - h
        kseg = np.zeros((768, KLEN[g]), NPBF)
        vseg = np.zeros((KLEN[g], 768), NPBF)
        valid = np.zeros((KLEN[g],), NPBF)
        s = max(lo, 0)
        kseg[:, s - lo:] = ckT[:, s:t0 + TOK]
        vseg[s - lo:] = cv[s:t0 + TOK]
        valid[s - lo:] = 1
        kk = kseg[g * 256:(g + 1) * 256].reshape(4, 64, KLEN[g])
        d[f"ck{g}"] = np.ascontiguousarray(kk.transpose(1, 0, 2).reshape(64, 4 * KLEN[g]))
        vv = vseg[:, g * 256:(g + 1) * 256].reshape(NBLK[g], 128, 4, 64)
        va = np.zeros((NBLK[g], 128, 4, 65), NPBF)
        va[..., :64] = vv
        va[..., 64] = valid.reshape(NBLK[g], 128)[:, :, None]
        d[f"cv{g}"] = np.ascontiguousarray(va.transpose(1, 0, 2, 3).reshape(128, NBLK[g] * 260))
    return d


TOK = 2048
NT = TOK // 128
S = 16384
NKB = S // 128
BIG = 30000.0


def core_gtiles(core):
    return [8 * i + core for i in range(NT)]


def nsa_consts(core):
    gts = core_gtiles(core)
    t = np.concatenate([g * 128 + np.arange(128) for g in gts])
    c = np.arange(1024)
    cmask = ((16 * c[None, :] + 31) <= t[:, None]) & (c[None, :] < 1023)
    n = np.arange(256)
    cur = (t // 64)[:, None]
    forced = (n[None, :] == cur) | (n[None, :] == 0)
    fut = (n[None, :] > cur) & ~forced
    keep = ~(forced | fut)
    add = 1e9 * forced - 1.0 * fut
    gt = (t // 128)[:, None]
    past = n[None, :] < 2 * gt
    cf = np.stack([keep.astype(np.float32), add.astype(np.float32), past.astype(np.float32)], 1)
    constF = np.ascontiguousarray(cf.reshape(NT, 128, 768).transpose(1, 0, 2).reshape(128, NT * 768))
    kk = np.arange(128)
    ed = np.zeros((NT, 2, 128, 128), np.float32)
    for i in range(NT):
        g = gts[i]
        for k in range(128):
            nb = 2 * g + k // 64
            ed[i, nb // 128, nb % 128, k] = 1.0
    cb = np.concatenate([cmask.reshape(NT, 128, 1024).astype(np.float32),
                         ed[:, 0].transpose(0, 1, 2), ed[:, 1]], axis=2)
    constB = np.ascontiguousarray(cb.transpose(1, 0, 2).reshape(128, NT * 1280)).astype(NPBF)
    return constF, constB


def nsa_static():
    kk = np.arange(128)[:, None]
    qq = np.arange(128)[None, :]
    trib = np.tile(np.where(kk > qq, -BIG, 0.0), (1, 4)).astype(np.float32)
    wb0 = np.tile(np.where(kk > qq, 0.0, -BIG), (1, 4)).astype(np.float32)
    eexp = np.zeros((128, 64, 128), np.float32)
    for jj in range(64):
        for k in range(128):
            eexp[2 * jj + k // 64, jj, k] = 1.0
    ident = np.eye(128, dtype=np.float32)
    st = np.concatenate([trib, wb0, ident, eexp.reshape(128, 64 * 128)], axis=1)
    return np.ascontiguousarray(st).astype(NPBF)


ST_W = 512 + 512 + 128 + 8192


def build_N(kvh_list=(0, 1, 2), tile_list=tuple(range(NT)), n_main=NKB, skip=()):
    nc = bass.Bass("TRN2", target_bir_lowering=False)
    dt = lambda name, shape, dtype, kind: nc.dram_tensor(name, shape, dtype, kind=kind).ap()
    qr = dt("qr", [64, NT * 3 * 512], BF16, "ExternalInput")
    kc = dt("kc", [64, 3 * 1024], BF16, "ExternalInput")
    vc = dt("vc", [128, 3 * 8 * 65], BF16, "ExternalInput")
    ks = dt("ks", [64, 3 * S], BF16, "ExternalInput")
    vs = dt("vs", [128, 3 * NKB * 65], BF16, "ExternalInput")
    kso = dt("kso", [64, 3 * TOK], BF16, "ExternalInput")
    vso = dt("vso", [128, 3 * NT * 65], BF16, "ExternalInput")
    kw = dt("kw", [64, 3 * NT * 640], BF16, "ExternalInput")
    vw = dt("vw", [128, 3 * NT * 5 * 65], BF16, "ExternalInput")
    gate = dt("gate", [128, NT * 36], F32, "ExternalInput")
    constF = dt("constF", [128, NT * 768], F32, "ExternalInput")
    constB = dt("constB", [128, NT * 1280], BF16, "ExternalInput")
    stat = dt("stat", [128, ST_W], BF16, "ExternalInput")
    obT = dt("obT", [768, TOK], BF16, "ExternalOutput")
    with ExitStack() as es:
        P = Prog(nc, es)
        sb = lambda name, shape, d: es.enter_context(nc.sbuf_tensor(name, shape, d))
        pst = lambda name, shape, d: es.enter_context(nc.psum_tensor(name, shape, d))
        stat_sb = sb("stat_sb", [128, ST_W], BF16)
        bstat = Buf("stat")
        for j in range(4):
            w = ST_W // 4
            P.dma("sp", stat_sb[:, j * w:(j + 1) * w], stat[:, j * w:(j + 1) * w], bstat)
        trib = stat_sb[:, 0:512]
        wb0 = stat_sb[:, 512:1024]
        ident = stat_sb[:, 1024:1152]
        eexp = lambda jj: stat_sb[:, 1152 + jj * 128:1152 + (jj + 1) * 128]
        gate_sb = sb("gate_sb", [128, NT * 36], F32)
        bgate = Buf("gate")
        P.dma("sp", gate_sb[:], gate[:, :], bgate)
        ks_sb = sb("ks_sb", [64, S], BF16)
        vs_sb = sb("vs_sb", [128, NKB * 65], BF16)
        kso_sb = sb("kso_sb", [64, TOK], BF16)
        vso_sb = sb("vso_sb", [128, NT * 65], BF16)
        kw_sb = sb("kw_sb", [64, NT * 640], BF16)
        vw_sb = sb("vw_sb", [128, NT * 5 * 65], BF16)
        kc_sb = sb("kc_sb", [64, 1024], BF16)
        vc_sb = sb("vc_sb", [128, 8 * 65], BF16)
        bks, bvs, bkso, bvso, bkw, bvw, bkc, bvc = [Buf(n) for n in ("ks", "vs", "kso", "vso", "kw", "vw", "kc", "vc")]
        q_sb = [sb(f"q_sb{i}", [64, 512], BF16) for i in range(2)]
        bq = [Buf(f"q{i}") for i in range(2)]
        cF = [sb(f"cF{i}", [128, 768], F32) for i in range(2)]
        bcF = [Buf(f"cF{i}") for i in range(2)]
        cB = [sb(f"cB{i}", [128, 1280], BF16) for i in range(2)]
        bcB = [Buf(f"cB{i}") for i in range(2)]
        e_sb = [sb(f"e_sb{i}", [128, 1024], F32) for i in range(2)]
        be = [Buf(f"e{i}") for i in range(2)]
        em16 = [sb(f"em16_{i}", [128, 1024], BF16) for i in range(2)]
        bem16 = [Buf(f"em16_{i}") for i in range(2)]
        PT = [sb(f"PT{i}", [128, 1024], BF16) for i in range(2)]
        bPT = [Buf(f"PT{i}") for i in range(2)]
        imp = sb("imp", [128, 1024], F32)
        bimp = Buf("imp")
        sm = sb("sm", [128, 8], F32)
        bsm = [Buf(f"sm{i}") for i in range(4)]
        imps = sb("imps", [128, 256], F32)
        impm = sb("impm", [128, 256], F32)
        imp2 = sb("imp2", [128, 256], F32)
        sel = sb("sel", [128, 256], F32)
        bimps, bimpm, bimp2, bsel = Buf("imps"), Buf("impm"), Buf("imp2"), Buf("sel")
        m8 = sb("m8", [128, 16], F32)
        bm8 = Buf("m8")
        selb = sb("selb", [128, 512], BF16)
        bselb = Buf("selb")
        sbr = [sb(f"sbr{i}", [128, 512], BF16) for i in range(4)]
        bsbr = [Buf(f"sbr{i}") for i in range(4)]
        pT = [sb(f"pT{i}", [128, 512], BF16) for i in range(3)]
        bpT = [Buf(f"pT{i}") for i in range(3)]
        cf = sb("cf", [128, 16], F32)
        bcf = Buf("cf")
        of32 = sb("of32", [128, 256], F32)
        bof = Buf("of32")
        ob16 = sb("ob16", [128, 256], BF16)
        bob = Buf("ob16")
        oT = [sb(f"oT{i}", [128, 256], BF16) for i in range(2)]
        boT = [Buf(f"oT{i}") for i in range(2)]
        psc = pst("psc", [128, 1024], F32)
        bpsc = Buf("psc")
        ptr = pst("ptr", [128, 1024], BF16)
        bptr = Buf("ptr")
        pO = [pst(f"pO{i}", [128, 512], F32) for i in range(3)]
        bpO = [Buf(f"pO{i}") for i in range(3)]
        ps = [pst(f"ps{i}", [128, 512], F32) for i in range(2)]
        bps = [Buf(f"ps{i}") for i in range(2)]
        bout = Buf("o_ob")
        it = 0
        ci = 0
        for kvh in kvh_list:
            for j in range(4):
                w = S // 4
                P.dma("sp", ks_sb[:, j * w:(j + 1) * w], ks[:, kvh * S + j * w:kvh * S + (j + 1) * w], bks)
                w = NKB * 65 // 4
                P.dma("sp", vs_sb[:, j * w:(j + 1) * w], vs[:, kvh * NKB * 65 + j * w:kvh * NKB * 65 + (j + 1) * w], bvs)
            P.dma("sp", kso_sb[:], kso[:, kvh * TOK:(kvh + 1) * TOK], bkso)
            P.dma("sp", vso_sb[:], vso[:, kvh * NT * 65:(kvh + 1) * NT * 65], bvso)
            P.dma("sp", kw_sb[:], kw[:, kvh * NT * 640:(kvh + 1) * NT * 640], bkw)
            P.dma("sp", vw_sb[:], vw[:, kvh * NT * 325:(kvh + 1) * NT * 325], bvw)
            P.dma("sp", kc_sb[:], kc[:, kvh * 1024:(kvh + 1) * 1024], bkc)
            P.dma("sp", vc_sb[:], vc[:, kvh * 520:(kvh + 1) * 520], bvc)
            for i in tile_list:
                b2 = (kvh * NT + i) % 2
                q, bq_ = q_sb[b2], bq[b2]
                P.dma("sp", q[:], qr[:, (i * 3 + kvh) * 512:(i * 3 + kvh + 1) * 512], bq_)
                cF_, bcF_ = cF[b2], bcF[b2]
                cB_, bcB_ = cB[b2], bcB[b2]
                P.dma("sp", cF_[:], constF[:, i * 768:(i + 1) * 768], bcF_)
                P.dma("sp", cB_[:], constB[:, i * 1280:(i + 1) * 1280], bcB_)
                Oc, Os, Ow = pO
                bOc, bOs, bOw = bpO
                for g in (range(4) if 'cmp' not in skip else ()):
                    for hh in range(2):
                        P.op("pe", lambda e, q=q, g=g, hh=hh: e.matmul(
                            psc[:, hh * 512:(hh + 1) * 512], q[:, g * 128:(g + 1) * 128],
                            kc_sb[:, hh * 512:(hh + 1) * 512], start=True, stop=True),
                            reads=[bq_, bkc], writes=[bpsc])
                    P.op("dve", lambda e: e.reduce_max(out=sm[:, 0:1], in_=psc[:], axis=AX.X),
                         reads=[bpsc], writes=[bsm[0]])
                    P.op("dve", lambda e: e.tensor_scalar_mul(out=sm[:, 1:2], in0=sm[:, 0:1], scalar1=-0.125),
                         reads=[bsm[0]], writes=[bsm[1]])
                    ee, bee = e_sb[ci % 2], be[ci % 2]
                    e16, be16 = em16[ci % 2], bem16[ci % 2]
                    PT_, bPT_ = PT[ci % 2], bPT[ci % 2]
                    ci += 1
                    P.op("act", lambda e, ee=ee: e.activation(out=ee[:], in_=psc[:], func=AF.Exp, bias=sm[:, 1:2], scale=0.125),
                         reads=[bpsc, bsm[1]], writes=[bee])
                    P.op("dve", lambda e, ee=ee, cB_=cB_: e.tensor_tensor(out=ee[:], in0=ee[:], in1=cB_[:, 0:1024], op=ALU.mult),
                         reads=[bcB_], writes=[bee])
                    P.op("dve", lambda e, ee=ee: e.reduce_sum(out=sm[:, 2:3], in_=ee[:], axis=AX.X),
                         reads=[bee], writes=[bsm[2]])
                    P.op("dve", lambda e: e.tensor_scalar_max(out=sm[:, 2:3], in0=sm[:, 2:3], scalar1=1e-30),
                         reads=[bsm[2]], writes=[bsm[2]])
                    P.op("dve", lambda e: e.reciprocal(out=sm[:, 3:4], in_=sm[:, 2:3]), reads=[bsm[2]], writes=[bsm[3]])
                    if g == 0:
                        P.op("dve", lambda e, ee=ee: e.tensor_scalar_mul(out=imp[:], in0=ee[:], scalar1=sm[:, 3:4]),
                             reads=[bee, bsm[3]], writes=[bimp])
                    else:
                        P.op("dve", lambda e, ee=ee: e.scalar_tensor_tensor(
                            out=imp[:], in0=ee[:], scalar=sm[:, 3:4], in1=imp[:], op0=ALU.mult, op1=ALU.add),
                            reads=[bee, bsm[3]], writes=[bimp])
                    P.op("dve", lambda e, ee=ee, e16=e16: e.tensor_copy(out=e16[:], in_=ee[:]), reads=[bee], writes=[be16])
                    for cbk in range(8):
                        P.op("pe", lambda e, e16=e16, cbk=cbk: e.transpose(
                            ptr[:, cbk * 128:(cbk + 1) * 128], e16[:, cbk * 128:(cbk + 1) * 128], ident),
                            reads=[be16, bstat], writes=[bptr])
                    P.op("act", lambda e, PT_=PT_: e.activation(out=PT_[:], in_=ptr[:], func=AF.Copy),
                         reads=[bptr], writes=[bPT_])
                    for cbk in range(8):
                        P.op("pe", lambda e, PT_=PT_, cbk=cbk, g=g: e.matmul(
                            Oc[:, g * 65:(g + 1) * 65], PT_[:, cbk * 128:(cbk + 1) * 128],
                            vc_sb[:, cbk * 65:(cbk + 1) * 65], start=(g == 0 and cbk == 0), stop=(cbk == 7),
                            skip_group_check=True), reads=[bPT_, bvc], writes=[bOc])
                P.op("dve", lambda e: e.tensor_reduce(out=imps[:], in_=imp[:].rearrange("p (n f) -> p n f", f=4),
                                                       axis=AX.X, op=ALU.add), reads=[bimp], writes=[bimps])
                P.op("dve", lambda e, cF_=cF_: e.tensor_tensor(out=impm[:], in0=imps[:], in1=cF_[:, 0:256], op=ALU.mult),
                     reads=[bimps, bcF_], writes=[bimpm])
                P.op("dve", lambda e, cF_=cF_: e.tensor_tensor(out=impm[:], in0=impm[:], in1=cF_[:, 256:512], op=ALU.add),
                     reads=[bcF_], writes=[bimpm])
                P.op("dve", lambda e: e.max(out=m8[:, 0:8], in_=impm[:]), reads=[bimpm], writes=[bm8])
                P.op("dve", lambda e: e.match_replace(out=imp2[:], in_to_replace=m8[:, 0:8], in_values=impm[:], imm_value=-2.0),
                     reads=[bimpm, bm8], writes=[bimp2])
                P.op("dve", lambda e: e.max(out=m8[:, 8:16], in_=imp2[:]), reads=[bimp2], writes=[bm8])
                P.op("dve", lambda e: e.tensor_scalar(out=sel[:], in0=impm[:], scalar1=m8[:, 15:16], scalar2=None, op0=ALU.is_ge),
                     reads=[bimpm, bm8], writes=[bsel])
                P.op("dve", lambda e: e.tensor_scalar(out=selb[:, 256:512], in0=sel[:], scalar1=-1.0, scalar2=BIG,
                                                       op0=ALU.add, op1=ALU.mult), reads=[bsel], writes=[bselb])
                P.op("dve", lambda e, cF_=cF_: e.tensor_tensor(out=sel[:], in0=sel[:], in1=cF_[:, 512:768], op=ALU.mult),
                     reads=[bcF_], writes=[bsel])
                P.op("dve", lambda e: e.tensor_scalar(out=selb[:, 0:256], in0=sel[:], scalar1=-1.0, scalar2=BIG,
                                                       op0=ALU.add, op1=ALU.mult), reads=[bsel], writes=[bselb])
                for k4 in range(4):
                    P.op("pe", lambda e, k4=k4: e.transpose(ptr[:, k4 * 128:(k4 + 1) * 128], selb[:, k4 * 128:(k4 + 1) * 128], ident),
                         reads=[bselb, bstat], writes=[bptr])
                for k4 in range(4):
                    for g in range(4):
                        eng = "act" if (g % 2 == 0) else "dve"
                        if eng == "act":
                            P.op("act", lambda e, k4=k4, g=g: e.activation(
                                out=sbr[k4][:, g * 128:(g + 1) * 128], in_=ptr[:, k4 * 128:(k4 + 1) * 128], func=AF.Copy),
                                reads=[bptr], writes=[bsbr[k4]])
                        else:
                            P.op("dve", lambda e, k4=k4, g=g: e.tensor_copy(
                                out=sbr[k4][:, g * 128:(g + 1) * 128], in_=ptr[:, k4 * 128:(k4 + 1) * 128]),
                                reads=[bptr], writes=[bsbr[k4]])
                blocks = [("s", j) for j in list(range(min(n_main, 8 * i + 8))) + [NKB]]
                if 'win' not in skip:
                    blocks += [("w", r) for r in range(5)]
                first_s = blocks[0][1]

                def emit_scores(kind, j):
                    nonlocal it
                    p, bp = ps[it % 2], bps[it % 2]
                    t_sb, bt = pT[it % 3], bpT[it % 3]
                    it += 1
                    if kind == "s" and j < NKB:
                        P.op("pe", lambda e, p=p, j=j, q=q: e.matmul(
                            p[:], ks_sb[:, j * 128:(j + 1) * 128], q[:], start=True, stop=False, skip_group_check=True),
                            reads=[bks, bq_], writes=[bp])
                        P.op("pe", lambda e, p=p, j=j: e.matmul(
                            p[:], eexp(j % 64), sbr[j // 64][:], start=False, stop=True, skip_group_check=True),
                            reads=[bstat, bsbr[j // 64]], writes=[bp])
                        vsl, bv = vs_sb[:, j * 65:(j + 1) * 65], bvs
                        O_, bO_, st, sp_ = Os, bOs, (j == first_s), False
                    elif kind == "s":
                        P.op("pe", lambda e, p=p, i=i, q=q: e.matmul(
                            p[:], kso_sb[:, i * 128:(i + 1) * 128], q[:], start=True, stop=False, skip_group_check=True),
                            reads=[bkso, bq_], writes=[bp])
                        P.op("pe", lambda e, p=p, cB_=cB_: e.matmul(
                            p[:], cB_[:, 1024:1152], sbr[2][:], start=False, stop=False, skip_group_check=True),
                            reads=[bcB_, bsbr[2]], writes=[bp])
                        P.op("pe", lambda e, p=p, cB_=cB_: e.matmul(
                            p[:], cB_[:, 1152:1280], sbr[3][:], start=False, stop=False, skip_group_check=True),
                            reads=[bcB_, bsbr[3]], writes=[bp])
                        P.op("pe", lambda e, p=p: e.matmul(
                            p[:], ident, trib, start=False, stop=True, skip_group_check=True),
                            reads=[bstat], writes=[bp])
                        vsl, bv = vso_sb[:, i * 65:(i + 1) * 65], bvso
                        O_, bO_, st, sp_ = Os, bOs, (j == first_s), True
                    else:
                        r = j
                        kb = i * 5 + r
                        last = r not in (0, 4)
                        P.op("pe", lambda e, p=p, kb=kb, q=q, last=last: e.matmul(
                            p[:], kw_sb[:, kb * 128:(kb + 1) * 128], q[:], start=True, stop=last, skip_group_check=True),
                            reads=[bkw, bq_], writes=[bp])
                        if r == 0:
                            P.op("pe", lambda e, p=p: e.matmul(p[:], ident, wb0, start=False, stop=True, skip_group_check=True),
                                 reads=[bstat], writes=[bp])
                        if r == 4:
                            P.op("pe", lambda e, p=p: e.matmul(p[:], ident, trib, start=False, stop=True, skip_group_check=True),
                                 reads=[bstat], writes=[bp])
                        vsl, bv = vw_sb[:, kb * 65:(kb + 1) * 65], bvw
                        O_, bO_, st, sp_ = Ow, bOw, (r == 0), (r == 4)
                    P.op("act", lambda e, p=p, t_sb=t_sb: e.activation(out=t_sb[:], in_=p[:], func=AF.Exp, scale=0.125),
                         reads=[bp], writes=[bt])
                    return (t_sb, bt, vsl, bv, O_, bO_, st, sp_)

                def emit_pv(ctx):
                    t_sb, bt, vsl, bv, O_, bO_, st, sp_ = ctx
                    for g in range(4):
                        P.op("pe", lambda e, t_sb=t_sb, g=g, vsl=vsl, O_=O_, st=st, sp_=sp_: e.matmul(
                            O_[:, g * 65:(g + 1) * 65], t_sb[:, g * 128:(g + 1) * 128], vsl,
                            start=(st and g == 0), stop=sp_, skip_group_check=True),
                            reads=[bt, bv], writes=[bO_])

                prev = None
                for kind, j in blocks:
                    cur = emit_scores(kind, j)
                    if prev is not None:
                        emit_pv(prev)
                    prev = cur
                emit_pv(prev)
                for b, (O_, bO_) in enumerate(((Oc, bOc), (Os, bOs), (Ow, bOw))):
                    o3 = O_[:, 0:260].rearrange("p (h e) -> p h e", e=65)
                    P.op("dve", lambda e, o3=o3: e.tensor_scalar_max(out=cf[:, 0:4], in0=o3[:, :, 64], scalar1=1e-30),
                         reads=[bO_], writes=[bcf])
                    P.op("dve", lambda e: e.reciprocal(out=cf[:, 0:4], in_=cf[:, 0:4]), reads=[bcf], writes=[bcf])
                    g0 = i * 36 + kvh * 12 + b
                    gsl = gate_sb[:, g0:g0 + 10:3]
                    P.op("dve", lambda e, b=b, gsl=gsl: e.tensor_tensor(
                        out=cf[:, 4 + 4 * b:8 + 4 * b], in0=cf[:, 0:4], in1=gsl, op=ALU.mult),
                        reads=[bcf, bgate], writes=[bcf])
                    for g in range(4):
                        dst = ob16 if b == 2 else of32
                        bdst = bob if b == 2 else bof
                        if b == 0:
                            P.op("dve", lambda e, O_=O_, g=g, b=b: e.tensor_scalar_mul(
                                out=of32[:, g * 64:(g + 1) * 64], in0=O_[:, g * 65:g * 65 + 64],
                                scalar1=cf[:, 4 + 4 * b + g:5 + 4 * b + g]), reads=[bO_, bcf], writes=[bof])
                        else:
                            P.op("dve", lambda e, O_=O_, g=g, b=b, dst=dst: e.scalar_tensor_tensor(
                                out=dst[:, g * 64:(g + 1) * 64], in0=O_[:, g * 65:g * 65 + 64],
                                scalar=cf[:, 4 + 4 * b + g:5 + 4 * b + g], in1=of32[:, g * 64:(g + 1) * 64],
                                op0=ALU.mult, op1=ALU.add), reads=[bO_, bcf, bof], writes=[bdst])
                for cc in range(2):
                    P.op("pe", lambda e, cc=cc: e.transpose(ptr[:, cc * 128:(cc + 1) * 128], ob16[:, cc * 128:(cc + 1) * 128], ident),
                         reads=[bob, bstat], writes=[bptr])
                o_t, bo_t = oT[b2], boT[b2]
                P.op("act", lambda e, o_t=o_t: e.activation(out=o_t[:], in_=ptr[:, 0:256], func=AF.Copy),
                     reads=[bptr], writes=[bo_t])
                for cc in range(2):
                    r0 = kvh * 256 + cc * 128
                    P.dma("sp", obT[r0:r0 + 128, i * 128:(i + 1) * 128], o_t[:, cc * 128:(cc + 1) * 128], bout, src=bo_t)
        P.finish([bout])
        P.emit()
    return nc


def nsa_host_inputs(qT, kcT, vcm, ksT, vsm, kwT, vwm, gate, core):
    gts = core_gtiles(core)
    idx = np.concatenate([g * 128 + np.arange(128) for g in gts])
    d = {}
    q = qT[:, idx].reshape(3, 4, 64, NT, 128)
    d["qr"] = np.ascontiguousarray(q.transpose(2, 3, 0, 1, 4).reshape(64, NT * 3 * 512))
    d["kc"] = np.ascontiguousarray(kcT.transpose(1, 0, 2).reshape(64, 3 * 1024))
    va = np.ones((1024, 3, 65), NPBF)
    va[:, :, :64] = vcm
    d["vc"] = np.ascontiguousarray(va.reshape(8, 128, 3, 65).transpose(1, 2, 0, 3).reshape(128, 3 * 8 * 65))
    d["ks"] = np.ascontiguousarray(ksT.reshape(3, 64, S).transpose(1, 0, 2).reshape(64, 3 * S))
    va = np.ones((S, 3, 65), NPBF)
    va[:, :, :64] = vsm.reshape(S, 3, 64)
    d["vs"] = np.ascontiguousarray(va.reshape(NKB, 128, 3, 65).transpose(1, 2, 0, 3).reshape(128, 3 * NKB * 65))
    d["kso"] = np.ascontiguousarray(ksT[:, idx].reshape(3, 64, TOK).transpose(1, 0, 2).reshape(64, 3 * TOK))
    d["vso"] = np.ascontiguousarray(
        va[idx].reshape(NT, 128, 3, 65).transpose(1, 2, 0, 3).reshape(128, 3 * NT * 65))
    pos = np.concatenate([g * 128 - 512 + np.arange(640) for g in gts])
    valid = pos >= 0
    pc = np.clip(pos, 0, None)
    kseg = kwT[:, pc]
    kseg[:, ~valid] = 0
    vseg = np.zeros((NT * 640, 3, 65), NPBF)
    vseg[:, :, :64] = vwm[pc].reshape(-1, 3, 64)
    vseg[:, :, 64] = 1
    vseg[~valid] = 0
    d["kw"] = np.ascontiguousarray(kseg.reshape(3, 64, NT * 640).transpose(1, 0, 2).reshape(64, 3 * NT * 640))
    d["vw"] = np.ascontiguousarray(vseg.reshape(NT * 5, 128, 3, 65).transpose(1, 2, 0, 3).reshape(128, 3 * NT * 5 * 65))
    d["gate"] = np.ascontiguousarray(gate[idx].reshape(NT, 128, 36).transpose(1, 0, 2).reshape(128, NT * 36))
    return d


def nsa_scatter_out(obT_list):
    out = np.zeros((768, S), obT_list[0].dtype)
    for c, o in enumerate(obT_list):
        for i, g in enumerate(core_gtiles(c)):
            out[:, g * 128:(g + 1) * 128] = o[:, i * 128:(i + 1) * 128]
    return out


NCORES = 8
N_CHUNK = 4
_NC_CACHE = {}
_DEBUG = {}


def _get_nc(name):
    if name == "A":
        return build_A()
    if name == "L":
        return build_L()
    if name == "D":
        return build_D()
    if isinstance(name, tuple) and name[0] == "N":
        return build_N()
    if name == "C0":
        return build_C(False)
    if name == "C1":
        return build_C(True)
    raise KeyError(name)


def _run(name, in_maps):
    nc = _get_nc(name)
    res = run_bass_kernel_spmd(nc, in_maps, core_ids=list(range(NCORES)))
    return res.results


def kernel(x, ffn1_norm, ffn1_w1, ffn1_w3, ffn1_w2, mix_norm, w_in, conv_w, conv_b,
           lru_wa, lru_ba, lru_wi, lru_bi, lru_lambda, cmp_pos_k, cmp_pos_v,
           cmp_k_w1, cmp_k_w2, cmp_v_w1, cmp_v_w2, w_up_a, w_up_b, w_up_c, w_out,
           ffn2_norm, ffn2_w1, ffn2_w3, ffn2_w2, final_norm, _layers=None, _debug=None):
    f32 = lambda a: np.ascontiguousarray(np.asarray(a, dtype=np.float32))
    inp = {k: f32(v) for k, v in dict(
        conv_w=conv_w, conv_b=conv_b, lru_wa=lru_wa, lru_ba=lru_ba, lru_wi=lru_wi, lru_bi=lru_bi,
        lru_lambda=lru_lambda, cmp_pos_k=cmp_pos_k, cmp_pos_v=cmp_pos_v, cmp_k_w1=cmp_k_w1,
        cmp_k_w2=cmp_k_w2, cmp_v_w1=cmp_v_w1, cmp_v_w2=cmp_v_w2).items()}
    x = f32(x)[0]
    Sx = x.shape[0]
    depth = ffn1_norm.shape[0]
    layers = range(depth) if _layers is None else _layers
    xT = [np.ascontiguousarray(x[c * TOK:(c + 1) * TOK].T) for c in range(NCORES)]
    db = dil_masks().reshape(128, 24 * 512)
    identb = np.eye(128, dtype=np.float32).astype(NPBF)
    stat = nsa_static()
    nconst = [nsa_consts(c) for c in range(NCORES)]
    gl = vec_pk(f32(final_norm))
    for l in layers:
        w13, w2 = ffn_layout(f32(ffn1_w1[l]), f32(ffn1_w3[l]), f32(ffn1_w2[l]))
        wfm, wtm = w_in_layout(f32(w_in[l]))
        gf, gm = vec_pk(f32(ffn1_norm[l])), vec_pk(f32(mix_norm[l]))
        rA = _run("A", [{"xT": xT[c], "w13": w13, "w2": w2, "gf": gf, "gm": gm, "wfm": wfm, "wtm": wtm}
                        for c in range(NCORES)])
        del w13, w2, wfm, wtm
        fm16 = np.concatenate([rA[c]["pfm16"] for c in range(NCORES)], axis=1)
        tm16 = np.concatenate([rA[c]["ptm16"] for c in range(NCORES)], axis=0)
        gate = np.concatenate([rA[c]["pgate"] for c in range(NCORES)], axis=0)
        ax = np.concatenate([rA[c]["pfm32"][0:768] for c in range(NCORES)], axis=1)
        axp = np.concatenate([np.zeros((768, 3), np.float32), ax], axis=1)
        qT, kcT, vcT = fm16[0:768], fm16[768:960], fm16[960:1152]
        ksT, kwT, cqT, ckT = fm16[1152:1344], fm16[1344:1536], fm16[1536:2304], fm16[2304:3072]
        vsm, vwm, cvm = tm16[:, 0:192], tm16[:, 192:384], tm16[:, 384:1152]
        lp = lru_host_inputs(inp, l)
        mapsL = []
        for c in range(NCORES):
            d = dict(lp)
            d["axh"] = np.ascontiguousarray(axp[:, c * TOK:c * TOK + TOK + 3])
            d.update(cmp_host_inputs(inp, l, kcT, vcT, c))
            mapsL.append(d)
        rL = _run("L", mapsL)
        del mapsL
        kcc = np.stack([rL[c]["kcp"].reshape(64, 3, 128) for c in range(NCORES)], 0)
        kcfull = np.ascontiguousarray(kcc.transpose(2, 1, 0, 3).reshape(3, 64, 1024))
        vcfull = np.concatenate([rL[c]["vcp"].reshape(128, 3, 64) for c in range(NCORES)], 0)
        ends = np.zeros((128, 6, 2, NCORES), np.float32)
        for c in range(NCORES):
            ends[:, :, 0, c] = rL[c]["pcum"][:, -1].reshape(6, 128).T
            ends[:, :, 1, c] = rL[c]["hloc"][:, -1].reshape(6, 128).T
        ends = np.ascontiguousarray(ends.reshape(128, 96))
        mapsD = []
        for c in range(NCORES):
            d = dil_host_inputs(cqT, ckT, cvm, c)
            d["db"] = db
            d["ident"] = identb
            mapsD.append(d)
        rD = _run("D", mapsD)
        del mapsD
        mapsN = []
        for c in range(NCORES):
            d = nsa_host_inputs(qT, kcfull, vcfull, ksT, vsm, kwT, vwm, gate, c)
            d["constF"], d["constB"] = nconst[c]
            d["stat"] = stat
            mapsN.append(d)
        rN = _run(("N",), mapsN)
        obg = nsa_scatter_out([rN[c]["obT"] for c in range(NCORES)])
        rN = [{"obT": np.ascontiguousarray(obg[:, c * TOK:(c + 1) * TOK])} for c in range(NCORES)]
        del mapsN
        w13, w2 = ffn_layout(f32(ffn2_w1[l]), f32(ffn2_w3[l]), f32(ffn2_w2[l]))
        wup = chunked(np.concatenate([f32(w_up_a[l]), f32(w_up_b[l]), f32(w_up_c[l])], axis=0))
        wo = chunked(f32(w_out[l]))
        gf2 = vec_pk(f32(ffn2_norm[l]))
        last = (l == depth - 1)
        mapsC = []
        for c in range(NCORES):
            cs = np.zeros((128, 8), np.float32)
            if c > 0:
                cs[:, c - 1] = 1.0
            mapsC.append({"xT": rA[c]["x1T"], "pfm32": rA[c]["pfm32"], "hloc": rL[c]["hloc"], "pcum": rL[c]["pcum"],
                          "ends": ends, "csel": cs, "obT": rN[c]["obT"], "odT": rD[c]["odT"], "wup": wup,
                          "wout": wo, "w13": w13, "w2": w2, "gf": gf2, "gl": gl})
        rC = _run("C1" if last else "C0", mapsC)
        if _debug is not None:
            _debug[l] = dict(rA=rA, rL=rL, rD=rD, rN=rN, rC=rC, kcfull=kcfull, vcfull=vcfull, fm16=fm16, tm16=tm16,
                             gate=gate, ax=ax)
        xT = [rC[c]["x2T"] for c in range(NCORES)]
        del rA, rL, rD, rN, mapsC
    out = np.concatenate([xT[c].T for c in range(NCORES)], axis=0)[None]
    return np.ascontiguousarray(out.astype(np.float32))
```
